# Optimizing a Trainium2 kernel written in Bass

```python
import jax, jax.numpy as jnp
from jax import lax
import numpy as np

D_MODEL = 1024
BATCH = 4
SEQ = 8192
DEPTH = 2

SSD_D_INNER = 1024
SSD_HEAD_DIM = 64
SSD_HEADS = SSD_D_INNER // SSD_HEAD_DIM
SSD_GROUPS = 4
SSD_STATE = 128
SSD_CONV = 4
SSD_CHUNK = 128
SSD_CONV_CH = SSD_D_INNER + 2 * SSD_GROUPS * SSD_STATE
S5_WIDTH = 1024
S5_GROUP = 16
S5_GROUPS = S5_WIDTH // S5_GROUP
S5_STATE = 64
MLA_HEADS = 16
MLA_NOPE = 64
MLA_ROPE = 32
MLA_V = 64
MLA_Q_RANK = 512
MLA_KV_RANK = 256
MLA_WIDTH = MLA_HEADS * MLA_V
ROPE_THETA = 10000.0
Q_BLOCK = 128
N_BRANCH = 3
EPS = 1e-6
NEG_INF = -1e30

IN_SIZES = (SSD_D_INNER, SSD_CONV_CH, SSD_HEADS, S5_WIDTH, S5_WIDTH,
            MLA_Q_RANK, MLA_KV_RANK, MLA_ROPE, MLA_WIDTH, N_BRANCH * D_MODEL)
D_IN = (SSD_D_INNER + SSD_CONV_CH + SSD_HEADS + S5_WIDTH + S5_WIDTH
        + MLA_Q_RANK + MLA_KV_RANK + MLA_ROPE + MLA_WIDTH + N_BRANCH * D_MODEL)

kernel_name = "hybrid_ssd_s5_mla_gated_parallel"


def rms_norm(x, g):
    xf = x.astype(jnp.float32)
    y = xf * lax.rsqrt(jnp.mean(xf * xf, axis=-1, keepdims=True) + EPS)
    return (y * g.astype(jnp.float32)).astype(x.dtype)


def split_columns(p):
    outs = []
    start = 0
    for n in IN_SIZES:
        outs.append(p[..., start:start + n])
        start += n
    return outs


def causal_dwconv(x, w, b):
    k = w.shape[0]
    y = lax.conv_general_dilated(x, w[:, None, :].astype(x.dtype), window_strides=(1,),
                                 padding=[(k - 1, 0)], dimension_numbers=('NWC', 'WIO', 'NWC'),
                                 feature_group_count=x.shape[-1])
    return y + b


def segsum(a):
    cs = jnp.cumsum(a, axis=-1)
    d = cs[..., :, None] - cs[..., None, :]
    n = a.shape[-1]
    mask = jnp.tril(jnp.ones((n, n), dtype=bool))
    return jnp.where(mask, d, -jnp.inf)


def ssd_chunked(x, dt, a, bmat, cmat):
    bsz, s, h, p = x.shape
    g, n = bmat.shape[-2:]
    r = h // g
    l = SSD_CHUNK
    nc = s // l
    xd = (x * dt[..., None]).reshape(bsz, nc, l, g, r, p)
    adt = (dt * a).reshape(bsz, nc, l, g, r).transpose(0, 1, 3, 4, 2)
    bc = bmat.reshape(bsz, nc, l, g, n)
    cc = cmat.reshape(bsz, nc, l, g, n)
    a_cs = jnp.cumsum(adt, axis=-1)
    cb = jnp.einsum('bclgn,bcsgn->bcgls', cc, bc)
    m = cb[:, :, :, None] * jnp.exp(segsum(adt))
    y_diag = jnp.einsum('bcgrls,bcsgrp->bclgrp', m, xd)
    decay_to_end = jnp.exp(a_cs[..., -1:] - a_cs).transpose(0, 1, 4, 2, 3)
    chunk_states = jnp.einsum('bclgn,bclgrp->cbgrpn', bc, xd * decay_to_end[..., None])
    chunk_decay = jnp.exp(a_cs[..., -1]).transpose(1, 0, 2, 3)

    def step(state, inp):
        st, dec = inp
        return state * dec[..., None, None] + st, state

    h0 = jnp.zeros((bsz, g, r, p, n), jnp.float32)
    _, prev = lax.scan(step, h0, (chunk_states, chunk_decay))
    decay_in = jnp.exp(a_cs).transpose(0, 1, 4, 2, 3)[..., None]
    y_off = jnp.einsum('bclgn,cbgrpn->bclgrp', cc, prev) * decay_in
    return (y_diag + y_off).reshape(bsz, s, h, p)


def ssd_branch(z, xbc, dt_raw, conv_w, conv_b, dt_bias, a_log, d_skip, norm_g):
    bsz, s, _ = z.shape
    xbc = jax.nn.silu(causal_dwconv(xbc, conv_w, conv_b)).astype(jnp.float32)
    xs = xbc[..., :SSD_D_INNER].reshape(bsz, s, SSD_HEADS, SSD_HEAD_DIM)
    bm = xbc[..., SSD_D_INNER:SSD_D_INNER + SSD_GROUPS * SSD_STATE].reshape(bsz, s, SSD_GROUPS, SSD_STATE)
    cm = xbc[..., SSD_D_INNER + SSD_GROUPS * SSD_STATE:].reshape(bsz, s, SSD_GROUPS, SSD_STATE)
    dt = jax.nn.softplus(dt_raw.astype(jnp.float32) + dt_bias.astype(jnp.float32))
    a = -jnp.exp(a_log.astype(jnp.float32))
    y = ssd_chunked(xs, dt, a, bm, cm) + d_skip.astype(jnp.float32)[:, None] * xs
    y = y.reshape(bsz, s, SSD_D_INNER) * jax.nn.silu(z.astype(jnp.float32))
    return rms_norm(y, norm_g).astype(z.dtype)


def complex_combine(e1, e2):
    a1r, a1i, b1r, b1i = e1
    a2r, a2i, b2r, b2i = e2
    return (a2r * a1r - a2i * a1i,
            a2r * a1i + a2i * a1r,
            a2r * b1r - a2i * b1i + b2r,
            a2r * b1i + a2i * b1r + b2i)


def s5_branch(u, z, log_step, lam_re, lam_im, b_re, b_im, c_re, c_im, d_skip, w_glu, b_glu):
    bsz, s, _ = u.shape
    f32 = jnp.float32
    uf = u.astype(f32).reshape(bsz, s, S5_GROUPS, S5_GROUP)
    step = jnp.exp(log_step.astype(f32))[:, None]
    lr, li = lam_re.astype(f32), lam_im.astype(f32)
    mag = jnp.exp(lr * step)
    ang = li * step
    abar_re, abar_im = mag * jnp.cos(ang), mag * jnp.sin(ang)
    den = lr * lr + li * li
    nr, ni = abar_re - 1.0, abar_im
    f_re = (nr * lr + ni * li) / den
    f_im = (ni * lr - nr * li) / den
    br, bi = b_re.astype(f32), b_im.astype(f32)
    bb_re = f_re[..., None] * br - f_im[..., None] * bi
    bb_im = f_re[..., None] * bi + f_im[..., None] * br
    bu_re = jnp.einsum('gph,bsgh->bsgp', bb_re, uf)
    bu_im = jnp.einsum('gph,bsgh->bsgp', bb_im, uf)
    a_re = jnp.broadcast_to(abar_re, (1, s, S5_GROUPS, S5_STATE))
    a_im = jnp.broadcast_to(abar_im, (1, s, S5_GROUPS, S5_STATE))
    _, _, xr, xi = lax.associative_scan(complex_combine, (a_re, a_im, bu_re, bu_im), axis=1)
    y = (jnp.einsum('ghp,bsgp->bsgh', c_re.astype(f32), xr)
         - jnp.einsum('ghp,bsgp->bsgh', c_im.astype(f32), xi))
    y = y + d_skip.astype(f32).reshape(S5_GROUPS, S5_GROUP) * uf
    y = y.reshape(bsz, s, S5_WIDTH)
    gl = jax.nn.gelu(y)
    y = gl * jax.nn.sigmoid(gl @ w_glu.astype(f32) + b_glu.astype(f32))
    y = y * jax.nn.silu(z.astype(f32))
    return y.astype(u.dtype)


def apply_rope(x, cos, sin):
    half = x.shape[-1] // 2
    x1, x2 = x[..., :half], x[..., half:]
    return jnp.concatenate([x1 * cos - x2 * sin, x1 * sin + x2 * cos], axis=-1)


def mla_branch(c_q, c_kv, k_rope, z, q_norm, w_uq, kv_norm, w_ukv, cos, sin):
    bsz, s, _ = c_q.shape
    q = (rms_norm(c_q, q_norm) @ w_uq).reshape(bsz, s, MLA_HEADS, MLA_NOPE + MLA_ROPE)
    q_nope = q[..., :MLA_NOPE]
    q_rope = apply_rope(q[..., MLA_NOPE:], cos[:, None, :], sin[:, None, :]).astype(q.dtype)
    kv = (rms_norm(c_kv, kv_norm) @ w_ukv).reshape(bsz, s, MLA_HEADS, MLA_NOPE + MLA_V)
    k_nope, v = kv[..., :MLA_NOPE], kv[..., MLA_NOPE:]
    k_r = apply_rope(k_rope, cos, sin).astype(q.dtype)
    scale = (MLA_NOPE + MLA_ROPE) ** -0.5
    nb = s // Q_BLOCK
    qn_b = q_nope.reshape(bsz, nb, Q_BLOCK, MLA_HEADS, MLA_NOPE).swapaxes(0, 1)
    qr_b = q_rope.reshape(bsz, nb, Q_BLOCK, MLA_HEADS, MLA_ROPE).swapaxes(0, 1)
    kpos = jnp.arange(s)

    def attend(blk):
        qn, qr, i = blk
        sc = (jnp.einsum('bqhd,bkhd->bhqk', qn, k_nope)
              + jnp.einsum('bqhr,bkr->bhqk', qr, k_r)).astype(jnp.float32) * scale
        qpos = i * Q_BLOCK + jnp.arange(Q_BLOCK)
        sc = jnp.where(kpos[None, :] <= qpos[:, None], sc, NEG_INF)
        pr = jax.nn.softmax(sc, axis=-1).astype(v.dtype)
        return jnp.einsum('bhqk,bkhd->bqhd', pr, v)

    o = lax.map(attend, (qn_b, qr_b, jnp.arange(nb)))
    o = o.swapaxes(0, 1).reshape(bsz, s, MLA_WIDTH)
    return (o * jax.nn.silu(z.astype(o.dtype))).astype(z.dtype)


def setup_inputs(seed: int = 0) -> dict:
    key = jax.random.key(seed)
    ks = jax.random.split(key, 32)
    f32 = jnp.float32

    def nrm(k, shape, scale):
        return jax.random.normal(k, shape, f32) * scale

    def gain(k, n):
        return 1.0 + 0.02 * jax.random.normal(k, (DEPTH, n), f32)

    dt0 = jnp.exp(jax.random.uniform(ks[4], (DEPTH, SSD_HEADS), f32, np.log(1e-3), np.log(1e-1)))
    lam_im = jnp.broadcast_to(jnp.pi * jnp.arange(S5_STATE, dtype=f32), (DEPTH, S5_GROUPS, S5_STATE))
    return {
        "x": jax.random.normal(ks[0], (BATCH, SEQ, D_MODEL), f32),
        "pre_norm": gain(ks[1], D_MODEL),
        "w_in": nrm(ks[2], (DEPTH, D_MODEL, D_IN), D_MODEL ** -0.5),
        "conv_w": nrm(ks[3], (DEPTH, SSD_CONV, SSD_CONV_CH), SSD_CONV ** -0.5),
        "conv_b": nrm(ks[5], (DEPTH, SSD_CONV_CH), 0.02),
        "dt_bias": dt0 + jnp.log(-jnp.expm1(-dt0)),
        "a_log": jnp.log(jax.random.uniform(ks[6], (DEPTH, SSD_HEADS), f32, 1.0, 16.0)),
        "d_ssd": gain(ks[7], SSD_HEADS),
        "ssd_norm": gain(ks[8], SSD_D_INNER),
        "w_a": nrm(ks[9], (DEPTH, SSD_D_INNER, D_MODEL), SSD_D_INNER ** -0.5),
        "s5_log_step": jax.random.uniform(ks[10], (DEPTH, S5_GROUPS), f32, np.log(1e-3), np.log(1e-1)),
        "s5_lambda_re": -0.5 + nrm(ks[11], (DEPTH, S5_GROUPS, S5_STATE), 0.01),
        "s5_lambda_im": lam_im,
        "s5_b_re": nrm(ks[12], (DEPTH, S5_GROUPS, S5_STATE, S5_GROUP), (2 * S5_GROUP) ** -0.5),
        "s5_b_im": nrm(ks[13], (DEPTH, S5_GROUPS, S5_STATE, S5_GROUP), (2 * S5_GROUP) ** -0.5),
        "s5_c_re": nrm(ks[14], (DEPTH, S5_GROUPS, S5_GROUP, S5_STATE), (2 * S5_STATE) ** -0.5),
        "s5_c_im": nrm(ks[15], (DEPTH, S5_GROUPS, S5_GROUP, S5_STATE), (2 * S5_STATE) ** -0.5),
        "s5_d": nrm(ks[16], (DEPTH, S5_WIDTH), 1.0),
        "w_glu": nrm(ks[17], (DEPTH, S5_WIDTH, S5_WIDTH), S5_WIDTH ** -0.5),
        "b_glu": nrm(ks[18], (DEPTH, S5_WIDTH), 0.02),
        "w_b": nrm(ks[19], (DEPTH, S5_WIDTH, D_MODEL), S5_WIDTH ** -0.5),
        "q_norm": gain(ks[20], MLA_Q_RANK),
        "w_uq": nrm(ks[21], (DEPTH, MLA_Q_RANK, MLA_HEADS * (MLA_NOPE + MLA_ROPE)), MLA_Q_RANK ** -0.5),
        "kv_norm": gain(ks[22], MLA_KV_RANK),
        "w_ukv": nrm(ks[23], (DEPTH, MLA_KV_RANK, MLA_HEADS * (MLA_NOPE + MLA_V)), MLA_KV_RANK ** -0.5),
        "w_c": nrm(ks[24], (DEPTH, MLA_WIDTH, D_MODEL), MLA_WIDTH ** -0.5),
        "w_o": nrm(ks[25], (DEPTH, D_MODEL, D_MODEL), D_MODEL ** -0.5),
        "post_norm": gain(ks[26], D_MODEL),
    }


def reference(x, pre_norm, w_in, conv_w, conv_b, dt_bias, a_log, d_ssd, ssd_norm, w_a,
              s5_log_step, s5_lambda_re, s5_lambda_im, s5_b_re, s5_b_im, s5_c_re, s5_c_im,
              s5_d, w_glu, b_glu, w_b, q_norm, w_uq, kv_norm, w_ukv, w_c, w_o, post_norm):
    bsz, s, _ = x.shape
    pos = jnp.arange(s, dtype=jnp.float32)
    inv_freq = ROPE_THETA ** (-jnp.arange(0, MLA_ROPE, 2, dtype=jnp.float32) / MLA_ROPE)
    ang = pos[:, None] * inv_freq[None, :]
    cos, sin = jnp.cos(ang), jnp.sin(ang)
    for l in range(DEPTH):
        h = rms_norm(x, pre_norm[l])
        proj = h @ w_in[l]
        z_a, xbc, dt_raw, u_b, z_b, c_q, c_kv, k_rope, z_c, gate_logits = split_columns(proj)
        y_a = ssd_branch(z_a, xbc, dt_raw, conv_w[l], conv_b[l], dt_bias[l], a_log[l],
                         d_ssd[l], ssd_norm[l]) @ w_a[l]
        y_b = s5_branch(u_b, z_b, s5_log_step[l], s5_lambda_re[l], s5_lambda_im[l],
                        s5_b_re[l], s5_b_im[l], s5_c_re[l], s5_c_im[l], s5_d[l],
                        w_glu[l], b_glu[l]) @ w_b[l]
        y_c = mla_branch(c_q, c_kv, k_rope, z_c, q_norm[l], w_uq[l], kv_norm[l], w_ukv[l],
                         cos, sin) @ w_c[l]
        gates = jax.nn.sigmoid(gate_logits.astype(jnp.float32)).reshape(bsz, s, N_BRANCH, D_MODEL)
        merged = gates[..., 0, :] * y_a + gates[..., 1, :] * y_b + gates[..., 2, :] * y_c
        out = merged.astype(x.dtype) @ w_o[l]
        x = x + rms_norm(out, post_norm[l]).astype(x.dtype)
    return x
```

```python
import contextlib
import numpy as np
import ml_dtypes
import concourse.bass as bass
import concourse.mybir as mybir
from concourse.bass_utils import run_bass_kernel_spmd

F32 = mybir.dt.float32
BF16 = mybir.dt.bfloat16
I32 = mybir.dt.int32
AF = mybir.ActivationFunctionType
ALU = mybir.AluOpType
AX = mybir.AxisListType
NPBF = ml_dtypes.bfloat16
NCORES = 8
NQ = 10


class KB:
    def __init__(self):
        self.nc = bass.Bass("TRN2", target_bir_lowering=False)
        self.es = contextlib.ExitStack()
        nc = self.nc
        self.eng = {"pe": nc.tensor, "dve": nc.vector, "act": nc.scalar,
                    "pool": nc.gpsimd, "sp": nc.sync}
        self.sems = {}
        for e in ("pe", "dve", "act", "pool"):
            self.sems[("c", e)] = self.es.enter_context(nc.semaphore("c_" + e))
        for q in ("sp", "pool", "act"):
            for i in range(NQ):
                self.sems[("d", q, i)] = self.es.enter_context(nc.semaphore(f"d_{q}{i}"))
        self.ccnt = {e: 0 for e in ("pe", "dve", "act", "pool")}
        self.dcnt = {q: [0] * NQ for q in ("sp", "pool", "act")}
        self.dnext = {q: 0 for q in ("sp", "pool", "act")}
        self.waited = {}
        self.last_w = {}
        self.readers = {}
        self.n_ins = 0
        self._uid = 0
        self.cur = self.es
        self.pfx = ""
        self.binds = {}

    def uid(self, p="t"):
        self._uid += 1
        return f"{p}{self._uid}"

    def sb(self, shape, dt, name=None):
        return self.cur.enter_context(self.nc.sbuf_tensor(name or self.uid("sb"), list(shape), dt))

    def ps(self, shape, dt=F32, name=None):
        return self.cur.enter_context(self.nc.psum_tensor(name or self.uid("ps"), list(shape), dt))

    def dram(self, name, shape, dt, kind):
        if name in self.binds:
            return self.binds[name]
        return self.nc.dram_tensor(self.pfx + name, list(shape), dt, kind=kind).ap()

    def internal(self, name, shape, dt):
        return self.nc.dram_tensor(name, list(shape), dt, kind="Internal").ap()

    @contextlib.contextmanager
    def scope(self, pfx="", binds=None):
        old = (self.cur, self.pfx, self.binds)
        st = contextlib.ExitStack()
        self.cur, self.pfx, self.binds = st, pfx, dict(binds or {})
        try:
            yield
        finally:
            self.barrier()
            st.close()
            self.cur, self.pfx, self.binds = old

    def barrier(self):
        for e in ("pe", "dve", "act", "pool", "sp"):
            for sk, sem in self.sems.items():
                if sk[0] == "c":
                    val = self.ccnt[sk[1]]
                    if sk[1] == e and e == "pe":
                        continue
                else:
                    val = self.dcnt[sk[1]][sk[2]] * 16
                if val and self.waited.get((e, sk), 0) < val:
                    self.eng[e].wait_ge(sem, val)
                    self.waited[(e, sk)] = val
        self.last_w.clear()
        self.readers.clear()

    def _deps(self, reads, writes):
        deps = []
        for k in reads:
            w = self.last_w.get(k)
            if w:
                deps.append(w)
        for k in writes:
            w = self.last_w.get(k)
            if w:
                deps.append(w)
            for sk, (val, eng) in self.readers.get(k, {}).items():
                deps.append((sk, val, eng))
        return deps

    def _wait(self, e, deps):
        need = {}
        for (sk, val, eng) in deps:
            if eng == e and e == "pe":
                continue
            if val > need.get(sk, 0):
                need[sk] = val
        for sk, val in need.items():
            if self.waited.get((e, sk), 0) >= val:
                continue
            self.eng[e].wait_ge(self.sems[sk], val)
            self.waited[(e, sk)] = val

    def _record(self, tok, reads, writes):
        sk, val, eng = tok
        for k in writes:
            self.last_w[k] = tok
            self.readers[k] = {}
        for k in reads:
            self.readers.setdefault(k, {})[sk] = (val, eng)

    def op(self, e, fn, reads=(), writes=()):
        def _isps(k):
            k0 = k if isinstance(k, str) else k[0]
            return isinstance(k0, str) and k0.startswith("ps")
        writes = list(writes) + [k for k in reads if _isps(k)]
        reads = [k for k in reads if not _isps(k)]
        self._wait(e, self._deps(reads, writes))
        ins = fn(self.eng[e])
        ins.then_inc(self.sems[("c", e)], 1)
        self.ccnt[e] += 1
        self.n_ins += 1
        self._record((("c", e), self.ccnt[e], e), reads, writes)
        return ins

    def dma(self, q, out, in_, reads=(), writes=(), **kw):
        i = self.dnext[q]
        self.dnext[q] = (i + 1) % NQ
        sk = ("d", q, i)
        deps = self._deps(reads, writes)
        prev = self.dcnt[q][i]
        if prev > 0:
            deps.append((sk, prev * 16, "dma"))
        self._wait(q, deps)
        self.eng[q].dma_start(out=out, in_=in_, **kw).then_inc(self.sems[sk], 16)
        self.dcnt[q][i] += 1
        self.n_ins += 1
        self._record((sk, self.dcnt[q][i] * 16, "dma"), reads, writes)

    def finish(self):
        for q in ("sp", "pool", "act"):
            for i in range(NQ):
                if self.dcnt[q][i]:
                    self.eng["sp"].wait_ge(self.sems[("d", q, i)], self.dcnt[q][i] * 16)
        for e in ("pe", "dve", "act", "pool"):
            if self.ccnt[e]:
                self.eng["sp"].wait_ge(self.sems[("c", e)], self.ccnt[e])
        self.es.close()
        return self.nc


def run_spmd(nc, in_maps):
    res = run_bass_kernel_spmd(nc, in_maps, core_ids=list(range(len(in_maps))))
    return res.results


def build_linear(K, ntok, segs, xmul=False, xgelu=False, xnorm=False):
    kb = KB()
    with kb.scope():
        emit_linear(kb, K, ntok, segs, xmul=xmul, xgelu=xgelu, xnorm=xnorm)
    return kb.finish()


def emit_xnorm(kb, xt, K, ntok, eps=1e-6):
    KT = K // 128
    X = kb.dram("x", [ntok, K], F32, "ExternalInput")
    G = kb.dram("g", [128, K], F32, "ExternalInput")
    IDN = kb.dram("ident", [128, 128], BF16, "ExternalInput")
    g = kb.sb([128, K], F32)
    ident = kb.sb([128, 128], BF16)
    kb.dma("sp", g[:, :], G[:, :], writes=["xn_g"])
    kb.dma("sp", ident[:, :], IDN[:, :], writes=["xn_id"])
    NB = 2
    xs = [kb.sb([128, K], F32) for _ in range(NB)]
    sq = [kb.sb([128, K], F32) for _ in range(NB)]
    ss = [kb.sb([128, 1], F32) for _ in range(NB)]
    yb = [kb.sb([128, K], BF16) for _ in range(NB)]
    pst = [kb.ps([128, 1024], BF16) for _ in range(2)]
    for t in range(ntok // 128):
        s = t % NB
        kb.dma("sp", xs[s][:, :], X[t * 128:(t + 1) * 128, :], writes=[("xn_x", s)])
        kb.op("act", lambda e: e.activation(out=sq[s][:, :], in_=xs[s][:, :], func=AF.Square, accum_out=ss[s][:, :]),
              reads=[("xn_x", s)], writes=[("xn_sq", s), ("xn_ss", s)])
        kb.op("dve", lambda e: e.tensor_scalar(out=ss[s][:, :], in0=ss[s][:, :], scalar1=1.0 / K, scalar2=eps,
                                               op0=ALU.mult, op1=ALU.add), reads=[("xn_ss", s)], writes=[("xn_ss", s)])
        kb.op("act", lambda e: e.activation(out=ss[s][:, :], in_=ss[s][:, :], func=AF.Sqrt),
              reads=[("xn_ss", s)], writes=[("xn_ss", s)])
        kb.op("dve", lambda e: e.reciprocal(out=ss[s][:, :], in_=ss[s][:, :]), reads=[("xn_ss", s)], writes=[("xn_ss", s)])
        kb.op("dve", lambda e: e.scalar_tensor_tensor(out=yb[s][:, :], in0=xs[s][:, :], scalar=ss[s][:, 0:1], in1=g[:, :],
                                                      op0=ALU.mult, op1=ALU.mult),
              reads=[("xn_x", s), ("xn_ss", s), "xn_g"], writes=[("xn_y", s)])
        for kt in range(KT):
            kb.op("pe", lambda e: e.transpose(out=pst[s][:, kt * 128:(kt + 1) * 128], in_=yb[s][:, kt * 128:(kt + 1) * 128],
                                              identity=ident[:, :]),
                  reads=[("xn_y", s), "xn_id"], writes=[("ps_xn", s)])
        kb.op("dve", lambda e: e.tensor_copy(out=xt[:, :, t * 128:(t + 1) * 128],
                                             in_=pst[s][:, 0:KT * 128].rearrange("p (k t) -> p k t", t=128)),
              reads=[("ps_xn", s)], writes=[("xt", kt) for kt in range(KT)])


def emit_linear(kb, K, ntok, segs, xmul=False, xgelu=False, xnorm=False):
    KT = K // 128
    NCH = ntok // 512
    xt = kb.sb([128, KT, ntok], BF16)
    if xnorm:
        with kb.scope(pfx=kb.pfx, binds=kb.binds):
            emit_xnorm(kb, xt, K, ntok)
    else:
        XT = kb.dram("xt", [K, ntok], BF16, "ExternalInput")
        for kt in range(KT):
            kb.dma("sp", xt[:, kt, :], XT[kt * 128:(kt + 1) * 128, :], writes=[("xt", kt)])
    if xmul:
        X2 = kb.dram("xm", [K, ntok], BF16, "ExternalInput")
        x2 = [kb.sb([128, ntok], BF16) for _ in range(2)]
        for kt in range(KT):
            s = kt % 2
            kb.dma("pool", x2[s][:, :], X2[kt * 128:(kt + 1) * 128, :], writes=[("x2", s)])
            kb.op("dve", lambda e: e.tensor_tensor(out=xt[:, kt, :], in0=xt[:, kt, :], in1=x2[s][:, :], op=ALU.mult),
                  reads=[("x2", s)], writes=[("xt", kt)])
    if xgelu:
        GC = min(2048, ntok)
        ga = [kb.sb([128, GC], F32) for _ in range(2)]
        gb = [kb.sb([128, GC], F32) for _ in range(2)]
        i = 0
        for kt in range(KT):
            for c0 in range(0, ntok, GC):
                s = i % 2
                i += 1
                xv = xt[:, kt, c0:c0 + GC]
                kb.op("dve", lambda e: e.tensor_tensor(out=ga[s][:, :], in0=xv, in1=xv, op=ALU.mult),
                      reads=[("xt", kt)], writes=[("ga", s)])
                kb.op("dve", lambda e: e.tensor_scalar(out=ga[s][:, :], in0=ga[s][:, :], scalar1=0.044715, scalar2=1.0,
                                                       op0=ALU.mult, op1=ALU.add),
                      reads=[("ga", s)], writes=[("ga", s)])
                kb.op("dve", lambda e: e.tensor_tensor(out=ga[s][:, :], in0=ga[s][:, :], in1=xv, op=ALU.mult),
                      reads=[("ga", s), ("xt", kt)], writes=[("ga", s)])
                kb.op("act", lambda e: e.activation(out=gb[s][:, :], in_=ga[s][:, :], func=AF.Sigmoid, scale=1.5957691216),
                      reads=[("ga", s)], writes=[("gb", s)])
                kb.op("dve", lambda e: e.tensor_tensor(out=xv, in0=gb[s][:, :], in1=xv, op=ALU.mult),
                      reads=[("gb", s)], writes=[("xt", kt)])
    xkeys = [("xt", kt) for kt in range(KT)]

    NW, NP, NO = 2, 4, 3
    wf = [kb.sb([128, KT, 128], F32) for _ in range(NW)]
    wb = [kb.sb([128, KT, 128], BF16) for _ in range(NW)]
    pss = [kb.ps([128, 512], F32) for _ in range(NP)]
    tmp = [kb.sb([128, 512], F32) for _ in range(NO)]
    ots = {}
    ets = [[kb.sb([128, 512], BF16) for _ in range(NO)] for _ in range(3)]
    bias_t = kb.sb([128, 1], F32)
    wi = pi = oi = 0
    FUNC = {"sigmoid": AF.Sigmoid, "silu": AF.Silu, None: AF.Copy}
    for seg in segs:
        nm, M = seg["name"], seg["M"]
        odt = seg.get("odt", BF16)
        func = seg.get("func")
        W = kb.dram("w_" + nm, [K, M], F32, "ExternalInput")
        Y = kb.dram("y_" + nm, [M, ntok], odt, "ExternalOutput")
        Bv = kb.dram("b_" + nm, [M, 1], F32, "ExternalInput") if seg.get("bias") else None
        muls = [kb.dram(f"m{j}_" + nm, [M, ntok], BF16, "ExternalInput") for j in range(len(seg.get("mul", [])))]
        addt = kb.dram("a_" + nm, [M, ntok], seg.get("adt", BF16), "ExternalInput") if seg.get("add") else None
        if odt not in ots:
            ots[odt] = [kb.sb([128, 512], odt) for _ in range(NO)]
        if addt is not None and ("add", seg.get("adt", BF16)) not in ots:
            ots[("add", seg.get("adt", BF16))] = [kb.sb([128, 512], seg.get("adt", BF16)) for _ in range(NO)]
        for m0 in range(0, M, 128):
            mw = min(128, M - m0)
            ws = wi % NW
            wi += 1
            kb.dma("sp", wf[ws][:, :, :mw], W[:, m0:m0 + mw].rearrange("(kt p) m -> p kt m", p=128),
                   writes=[("wf", ws)])
            kb.op("pool", lambda e: e.tensor_copy(out=wb[ws][:, :, :mw], in_=wf[ws][:, :, :mw]),
                  reads=[("wf", ws)], writes=[("wb", ws)])
            if Bv is not None:
                kb.dma("sp", bias_t[:mw, :], Bv[m0:m0 + mw, :], writes=["bias"])
            for c in range(NCH):
                cs = slice(c * 512, (c + 1) * 512)
                p = pi % NP
                pi += 1
                o = oi % NO
                oi += 1
                ps = pss[p]
                for kt in range(KT):
                    kb.op("pe", lambda e: e.matmul(ps[:mw, :], lhsT=wb[ws][:, kt, :mw], rhs=xt[:, kt, cs],
                                                   start=(kt == 0), stop=(kt == KT - 1)),
                          reads=[("wb", ws), xkeys[kt]], writes=[("ps", p)])
                mulx = bool(seg.get("mulx"))
                extra = len(muls) + (1 if addt is not None else 0) + (1 if mulx else 0)
                ot = ots[odt][o]
                first_out = ot if extra == 0 else tmp[o]
                fo_key = ("ot", odt, o) if extra == 0 else ("tmp", o)
                bias_arg = bias_t[:mw, :] if Bv is not None else 0.0
                brd = ["bias"] if Bv is not None else []
                if func == "softplus":
                    kb.op("act", lambda e: e.activation(out=tmp[o][:mw, :], in_=ps[:mw, :], func=AF.Exp, bias=bias_arg),
                          reads=[("ps", p)] + brd, writes=[("tmp", o)])
                    kb.op("act", lambda e: e.activation(out=first_out[:mw, :], in_=tmp[o][:mw, :], func=AF.Ln, bias=1.0),
                          reads=[("tmp", o)], writes=[fo_key])
                elif func is None and Bv is None:
                    kb.op("dve", lambda e: e.tensor_copy(out=first_out[:mw, :], in_=ps[:mw, :]),
                          reads=[("ps", p)], writes=[fo_key])
                else:
                    f = AF.Identity if func is None else FUNC[func]
                    kb.op("act", lambda e: e.activation(out=first_out[:mw, :], in_=ps[:mw, :], func=f, bias=bias_arg),
                          reads=[("ps", p)] + brd, writes=[fo_key])
                nleft = extra
                if mulx:
                    nleft -= 1
                    dst, dk = (ot, ("ot", odt, o)) if nleft == 0 else (tmp[o], ("tmp", o))
                    kb.op("dve", lambda e: e.tensor_tensor(out=dst[:mw, :], in0=tmp[o][:mw, :], in1=xt[:mw, m0 // 128, cs],
                                                           op=ALU.mult),
                          reads=[("tmp", o), ("xt", m0 // 128)], writes=[dk])
                for j, mt in enumerate(muls):
                    et = ets[j][o]
                    kb.dma("pool", et[:mw, :], mt[m0:m0 + mw, cs], writes=[("et", j, o)])
                    nleft -= 1
                    dst, dk = (ot, ("ot", odt, o)) if nleft == 0 else (tmp[o], ("tmp", o))
                    kb.op("dve", lambda e: e.tensor_tensor(out=dst[:mw, :], in0=tmp[o][:mw, :], in1=et[:mw, :], op=ALU.mult),
                          reads=[("tmp", o), ("et", j, o)], writes=[dk])
                if addt is not None:
                    at = ots[("add", seg.get("adt", BF16))][o]
                    kb.dma("pool", at[:mw, :], addt[m0:m0 + mw, cs], writes=[("at", o)])
                    kb.op("dve", lambda e: e.tensor_tensor(out=ot[:mw, :], in0=tmp[o][:mw, :], in1=at[:mw, :], op=ALU.add),
                          reads=[("tmp", o), ("at", o)], writes=[("ot", odt, o)])
                kb.dma("sp", Y[m0:m0 + mw, cs], ot[:mw, :], reads=[("ot", odt, o)], writes=[("Y", nm, m0, c)])


def build_rmsnorm(ntok, D, xdt=F32, premul=False, resid=False, odt=BF16, eps=1e-6):
    kb = KB()
    NT = ntok // 128
    X = kb.dram("x", [ntok, D], xdt, "ExternalInput")
    G = kb.dram("g", [128, D], F32, "ExternalInput")
    Y = kb.dram("y", [ntok, D], odt, "ExternalOutput")
    PM = kb.dram("pm", [ntok, D], BF16, "ExternalInput") if premul else None
    R = kb.dram("r", [ntok, D], F32, "ExternalInput") if resid else None
    g = kb.sb([128, D], F32)
    kb.dma("sp", g[:, :], G[:, :], writes=["g"])
    NB = 3
    xs = [kb.sb([128, D], xdt) for _ in range(NB)]
    xf = [kb.sb([128, D], F32) for _ in range(NB)]
    pm = [kb.sb([128, D], BF16) for _ in range(NB)] if premul else None
    rs = [kb.sb([128, D], F32) for _ in range(NB)] if resid else None
    sq = [kb.sb([128, D], F32) for _ in range(NB)]
    ss = [kb.sb([128, 1], F32) for _ in range(NB)]
    ys = [kb.sb([128, D], odt) for _ in range(NB)]
    for t in range(NT):
        s = t % NB
        rows = slice(t * 128, (t + 1) * 128)
        kb.dma("sp", xs[s][:, :], X[rows, :], writes=[("xs", s)])
        src, sk = xs[s], ("xs", s)
        if premul:
            kb.dma("pool", pm[s][:, :], PM[rows, :], writes=[("pm", s)])
            kb.op("dve", lambda e: e.tensor_tensor(out=xf[s][:, :], in0=xs[s][:, :], in1=pm[s][:, :], op=ALU.mult),
                  reads=[("xs", s), ("pm", s)], writes=[("xf", s)])
            src, sk = xf[s], ("xf", s)
        if resid:
            kb.dma("pool", rs[s][:, :], R[rows, :], writes=[("rs", s)])
        kb.op("act", lambda e: e.activation(out=sq[s][:, :], in_=src[:, :], func=AF.Square, accum_out=ss[s][:, :]),
              reads=[sk], writes=[("sq", s), ("ss", s)])
        kb.op("dve", lambda e: e.tensor_scalar(out=ss[s][:, :], in0=ss[s][:, :], scalar1=1.0 / D, scalar2=eps,
                                               op0=ALU.mult, op1=ALU.add),
              reads=[("ss", s)], writes=[("ss", s)])
        kb.op("act", lambda e: e.activation(out=ss[s][:, :], in_=ss[s][:, :], func=AF.Sqrt),
              reads=[("ss", s)], writes=[("ss", s)])
        kb.op("dve", lambda e: e.reciprocal(out=ss[s][:, :], in_=ss[s][:, :]),
              reads=[("ss", s)], writes=[("ss", s)])
        if resid:
            kb.op("dve", lambda e: e.scalar_tensor_tensor(out=sq[s][:, :], in0=src[:, :], scalar=ss[s][:, 0:1], in1=g[:, :],
                                                          op0=ALU.mult, op1=ALU.mult),
                  reads=[sk, ("ss", s), "g"], writes=[("sq", s)])
            kb.op("dve", lambda e: e.tensor_tensor(out=ys[s][:, :], in0=sq[s][:, :], in1=rs[s][:, :], op=ALU.add),
                  reads=[("sq", s), ("rs", s)], writes=[("ys", s)])
        else:
            kb.op("dve", lambda e: e.scalar_tensor_tensor(out=ys[s][:, :], in0=src[:, :], scalar=ss[s][:, 0:1], in1=g[:, :],
                                                          op0=ALU.mult, op1=ALU.mult),
                  reads=[sk, ("ss", s), "g"], writes=[("ys", s)])
        kb.dma("sp", Y[rows, :], ys[s][:, :], reads=[("ys", s)], writes=[("Y", t)])
    return kb.finish()


def build_attn(S, NH, DK, DV, scale):
    kb = KB()
    QT = kb.dram("qt", [NH, DK, S], BF16, "ExternalInput")
    KTd = kb.dram("kt", [NH, DK, S], BF16, "ExternalInput")
    V = kb.dram("v", [S, NH * DV], BF16, "ExternalInput")
    MK = kb.dram("mask", [128, 4, 512], BF16, "ExternalInput")
    ID = kb.dram("ident", [128, 128], BF16, "ExternalInput")
    SEL = kb.dram("sel", [DV + 1, DV], F32, "ExternalInput")
    OT = kb.dram("ot", [NH, DV, S], BF16, "ExternalOutput")
    NKT = S // 128
    NQC = S // 512
    mask = kb.sb([128, 4, 512], BF16)
    ident = kb.sb([128, 128], BF16)
    sel = kb.sb([DV + 1, DV], F32)
    kb.dma("sp", mask[:, :, :], MK[:, :, :], writes=["mask"])
    kb.dma("sp", ident[:, :], ID[:, :], writes=["ident"])
    kb.dma("sp", sel[:, :], SEL[:, :], writes=["sel"])
    qs = [kb.sb([DK, S], BF16) for _ in range(2)]
    ks = [kb.sb([DK, S], BF16) for _ in range(2)]
    vs = [kb.sb([128, NKT, DV + 1], BF16) for _ in range(2)]
    for b in range(2):
        kb.op("pool", lambda e: e.memset(vs[b][:, :, DV:DV + 1], 1.0), writes=[("v", b)])
    NPS, NPP = 4, 4
    ps_s = [kb.ps([128, 512], F32) for _ in range(NPS)]
    ps_o = [kb.ps([128, 512], F32) for _ in range(2)]
    ps_b = kb.ps([128, 512], F32)
    pt = [kb.sb([128, 512], BF16) for _ in range(NPP)]
    osb = [kb.sb([DV + 1, 512], F32) for _ in range(2)]
    rb = [kb.sb([DV, 512], F32) for _ in range(2)]
    oo = [kb.sb([DV, 512], BF16) for _ in range(2)]
    Vv = V.rearrange("(kt p) c -> p kt c", p=128)
    tasks = []
    for h in range(NH):
        for qc in range(NQC):
            nk = 4 * qc + 4
            for kt in range(nk):
                tasks.append((h, qc, kt, nk))
    LA = 2
    n = len(tasks)
    loaded = set()

    def load_head(h):
        if h in loaded or h >= NH:
            return
        loaded.add(h)
        b = h % 2
        kb.dma("sp", qs[b][:, :], QT[h, :, :], writes=[("q", b)])
        kb.dma("sp", ks[b][:, :], KTd[h, :, :], writes=[("k", b)])
        VG = min(16, NKT)
        for k4 in range(0, NKT, VG):
            kb.dma("pool", vs[b][:, k4:k4 + VG, 0:DV], Vv[:, k4:k4 + VG, h * DV:(h + 1) * DV], writes=[("v", b)])

    def emit_qk(i):
        h, qc, kt, nk = tasks[i]
        b = h % 2
        s_ = i % NPS
        p = i % NPP
        d = kt - 4 * qc
        qsl = slice(qc * 512, (qc + 1) * 512)
        kb.op("pe", lambda e: e.matmul(ps_s[s_][:, :], lhsT=ks[b][:, kt * 128:(kt + 1) * 128], rhs=qs[b][:, qsl],
                                       start=True, stop=(d < 0)),
              reads=[("k", b), ("q", b)], writes=[("ps_s", s_)])
        if d >= 0:
            kb.op("pe", lambda e: e.matmul(ps_s[s_][:, :], lhsT=ident[:, :], rhs=mask[:, d, :], start=False, stop=True),
                  reads=["ident", "mask"], writes=[("ps_s", s_)])
        kb.op("act", lambda e: e.activation(out=pt[p][:, :], in_=ps_s[s_][:, :], func=AF.Exp, scale=scale),
              reads=[("ps_s", s_)], writes=[("pt", p)])

    def emit_pv(i):
        h, qc, kt, nk = tasks[i]
        b = h % 2
        p = i % NPP
        ob = (h * NQC + qc) % 2
        qsl = slice(qc * 512, (qc + 1) * 512)
        kb.op("pe", lambda e: e.matmul(ps_o[ob][:DV + 1, :], lhsT=vs[b][:, kt, :], rhs=pt[p][:, :],
                                       start=(kt == 0), stop=(kt == nk - 1)),
              reads=[("v", b), ("pt", p)], writes=[("ps_o", ob)])
        if kt == nk - 1:
            kb.op("dve", lambda e: e.tensor_copy(out=osb[ob][:, :], in_=ps_o[ob][:DV + 1, :]),
                  reads=[("ps_o", ob)], writes=[("osb", ob)])
            kb.op("pe", lambda e: e.matmul(ps_b[:DV, :], lhsT=sel[:, :], rhs=osb[ob][:, :], start=True, stop=True),
                  reads=["sel", ("osb", ob)], writes=["ps_b"])
            kb.op("dve", lambda e: e.reciprocal(out=rb[ob][:, :], in_=ps_b[:DV, :]),
                  reads=["ps_b"], writes=[("rb", ob)])
            kb.op("dve", lambda e: e.tensor_tensor(out=oo[ob][:, :], in0=osb[ob][:DV, :], in1=rb[ob][:, :], op=ALU.mult),
                  reads=[("osb", ob), ("rb", ob)], writes=[("oo", ob)])
            kb.dma("sp", OT[h, :, qsl], oo[ob][:, :], reads=[("oo", ob)], writes=[("OT", h, qc)])

    load_head(0)
    for i in range(n + LA):
        if i < n:
            h, qc, kt, nk = tasks[i]
            if qc == 0 and kt == 0:
                load_head(h)
            emit_qk(i)
        if i - LA >= 0:
            emit_pv(i - LA)
    return kb.finish()


def attn_consts(DV=64):
    p = np.arange(128)[:, None, None]
    d = np.arange(4)[None, :, None]
    f = np.arange(512)[None, None, :]
    mask = np.where(f >= d * 128 + p, 0.0, -30000.0).astype(NPBF)
    ident = np.eye(128, dtype=np.float32).astype(NPBF)
    sel = np.zeros((DV + 1, DV), np.float32)
    sel[DV, :] = 1.0
    return {"mask": mask, "ident": ident, "sel": sel}


def build_ssd(S, NHD=8, NG=2, P=64, N=128, SEG=2048, dbg=99):
    kb = KB()
    L = 128
    NCK = S // L
    HPG = NHD // NG
    CX = NHD * P
    NCH = CX + 2 * NG * N
    NT = NCH // 128
    XBC = kb.dram("xbc", [NCH, S], BF16, "ExternalInput")
    CW = kb.dram("convw", [128, NT, 4], F32, "ExternalInput")
    CB = kb.dram("convb", [128, NT, 1], F32, "ExternalInput")
    DTT = kb.dram("dtT", [NHD, S], F32, "ExternalInput")
    DTM = kb.dram("dtm", [128, NCK, NHD], F32, "ExternalInput")
    ALC = kb.dram("alog_col", [NHD, 1], F32, "ExternalInput")
    ALB = kb.dram("alog_bc", [128, NHD], F32, "ExternalInput")
    DSK = kb.dram("dskip_bc", [128, NHD], F32, "ExternalInput")
    RMK = kb.dram("rmask", [NHD, SEG], F32, "ExternalInput")
    TRI = kb.dram("tri", [128, 128], F32, "ExternalInput")
    ONE = kb.dram("ones", [128, 128], F32, "ExternalInput")
    SELH = kb.dram("selh", [NHD, NHD, 128], F32, "ExternalInput")
    IDN = kb.dram("ident", [128, 128], BF16, "ExternalInput")
    Y = kb.dram("y", [S, CX], BF16, "ExternalOutput")

    def load(dr, shape, dt, key):
        t = kb.sb(shape, dt)
        idx = tuple(slice(None) for _ in shape)
        kb.dma("sp", t[idx], dr[idx], writes=[key])
        return t
    cw = kb.sb([128, NT, 4], F32)
    cb = kb.sb([128, NT, 1], F32)
    kb.dma("sp", cw[:, :, :], CW[:, :, :], writes=["cw"])
    kb.dma("sp", cb[:, :, :], CB[:, :, :], writes=["cb"])
    dtm = load(DTM, [128, NCK, NHD], F32, "dtm")
    alc = load(ALC, [NHD, 1], F32, "alc")
    alb = load(ALB, [128, NHD], F32, "alb")
    dsk = load(DSK, [128, NHD], F32, "dsk")
    rmk = load(RMK, [NHD, SEG], F32, "rmk")
    tri = load(TRI, [128, 128], F32, "tri")
    one = load(ONE, [128, 128], F32, "one")
    selh = load(SELH, [NHD, NHD, 128], F32, "selh")
    ident = load(IDN, [128, 128], BF16, "ident")
    kb.op("act", lambda e: e.activation(out=alc[:, :], in_=alc[:, :], func=AF.Exp), reads=["alc"], writes=["alc"])
    kb.op("dve", lambda e: e.tensor_scalar(out=alc[:, :], in0=alc[:, :], scalar1=-1.0, scalar2=None, op0=ALU.mult),
          reads=["alc"], writes=["alc"])
    kb.op("act", lambda e: e.activation(out=alb[:, :], in_=alb[:, :], func=AF.Exp), reads=["alb"], writes=["alb"])
    kb.op("dve", lambda e: e.tensor_scalar(out=alb[:, :], in0=alb[:, :], scalar1=-1.0, scalar2=None, op0=ALU.mult),
          reads=["alb"], writes=["alb"])
    NCH8 = NCK * NHD
    adtm = kb.sb([128, NCK, NHD], F32)
    cscol = kb.sb([128, NCH8], F32)
    ce = kb.sb([128, NCH8], F32)
    dte = kb.sb([128, NCK, NHD], F32)
    dece = kb.sb([128, NCK, NHD], F32)
    xdsc = kb.sb([128, NCK, NHD], F32)
    kb.op("dve", lambda e: e.tensor_tensor(out=adtm[:, :, :], in0=dtm[:, :, :],
                                           in1=alb[:, :].unsqueeze(1).to_broadcast([128, NCK, NHD]), op=ALU.mult),
          reads=["dtm", "alb"], writes=["adtm"])
    ps_y = kb.ps([128, CX], F32)
    ps_st = kb.ps([128, CX], F32)
    ps_a, ps_b2 = ps_y, ps_st
    adf = adtm[:, :, :].rearrange("p c h -> p (c h)")
    for c0 in range(0, NCH8, 512):
        cw_ = min(512, NCH8 - c0)
        kb.op("pe", lambda e: e.matmul(ps_a[:, :cw_], lhsT=tri[:, :], rhs=adf[:, c0:c0 + cw_], start=True, stop=True),
              reads=["tri", "adtm"], writes=["ps_y"])
        kb.op("dve", lambda e: e.tensor_copy(out=cscol[:, c0:c0 + cw_], in_=ps_a[:, :cw_]), reads=["ps_y"], writes=["cscol"])
        kb.op("pe", lambda e: e.matmul(ps_b2[:, :cw_], lhsT=one[:, :], rhs=adf[:, c0:c0 + cw_], start=True, stop=True),
              reads=["one", "adtm"], writes=["ps_st"])
        kb.op("dve", lambda e: e.tensor_copy(out=ce[:, c0:c0 + cw_], in_=ps_b2[:, :cw_]), reads=["ps_st"], writes=["ce"])
    dtef = dte[:, :, :].rearrange("p c h -> p (c h)")
    decef = dece[:, :, :].rearrange("p c h -> p (c h)")
    kb.op("dve", lambda e: e.tensor_tensor(out=dtef, in0=ce[:, :], in1=cscol[:, :], op=ALU.subtract),
          reads=["ce", "cscol"], writes=["dte"])
    kb.op("act", lambda e: e.activation(out=dtef, in_=dtef, func=AF.Exp), reads=["dte"], writes=["dte"])
    kb.op("act", lambda e: e.activation(out=decef, in_=ce[:, :], func=AF.Exp), reads=["ce"], writes=["dece"])
    kb.op("dve", lambda e: e.tensor_tensor(out=xdsc[:, :, :], in0=dtm[:, :, :], in1=dte[:, :, :], op=ALU.mult),
          reads=["dtm", "dte"], writes=["xdsc"])

    if dbg <= 1:
        return kb.finish()
    xin = [kb.sb([128, SEG + 3], BF16) for _ in range(2)]
    acc = [kb.sb([128, SEG], F32) for _ in range(2)]
    cv = [kb.sb([128, SEG], BF16) for _ in range(NT)]
    dts = kb.sb([NHD, SEG], F32)
    adts = kb.sb([NHD, SEG], F32)
    css = kb.sb([NHD, SEG], F32)
    S32 = kb.sb([128, NHD, P], F32)
    Sbf = kb.sb([128, NHD, P], BF16)
    kb.op("dve", lambda e: e.memset(S32[:, :, :], 0.0), writes=["S32"])
    kb.op("dve", lambda e: e.memset(Sbf[:, :, :], 0.0), writes=["Sbf"])
    ps_x = kb.ps([128, 1024], BF16)
    ps_bt = kb.ps([128, 1024], BF16)
    ps_cb = kb.ps([128, 4, 128], F32)
    ps_bc = [kb.ps([128, 4, 128], F32) for _ in range(2)]
    xd = kb.sb([128, NHD, P], BF16)
    xdd = kb.sb([128, NHD, P], BF16)
    xD = kb.sb([128, NHD, P], F32)
    btm = kb.sb([128, NG * N], BF16)
    cbm = kb.sb([128, NG, 128], F32)
    NB = 3
    Dm = [kb.sb([128, 128], F32) for _ in range(NB)]
    Em = [kb.sb([128, 128], F32) for _ in range(NB)]
    Ei = [kb.sb([128, 128], F32) for _ in range(NB)]
    MT = [kb.sb([128, 128], BF16) for _ in range(NB)]
    Cd = [kb.sb([128, 128], BF16) for _ in range(NB)]
    yo = [kb.sb([128, CX], BF16) for _ in range(2)]
    hi = 0
    for sg in range(S // SEG):
        t0 = sg * SEG
        for t in range(NT):
            b = t % 2
            rows = slice(t * 128, (t + 1) * 128)
            if sg == 0:
                kb.op("pool", lambda e: e.memset(xin[b][:, 0:3], 0.0), writes=[("xin", b)])
                kb.dma("sp", xin[b][:, 3:], XBC[rows, 0:SEG], writes=[("xin", b)])
            else:
                kb.dma("sp", xin[b][:, :], XBC[rows, t0 - 3:t0 + SEG], writes=[("xin", b)])
            kb.op("dve", lambda e: e.tensor_scalar(out=acc[b][:, :], in0=xin[b][:, 0:SEG], scalar1=cw[:, t, 0:1],
                                                   scalar2=None, op0=ALU.mult),
                  reads=[("xin", b), "cw"], writes=[("acc", b)])
            for k in range(1, 4):
                kb.op("dve", lambda e: e.scalar_tensor_tensor(out=acc[b][:, :], in0=xin[b][:, k:k + SEG],
                                                              scalar=cw[:, t, k:k + 1], in1=acc[b][:, :],
                                                              op0=ALU.mult, op1=ALU.add),
                      reads=[("xin", b), "cw", ("acc", b)], writes=[("acc", b)])
            kb.op("act", lambda e: e.activation(out=cv[t][:, :], in_=acc[b][:, :], func=AF.Silu, bias=cb[:, t, :]),
                  reads=[("acc", b), "cb"], writes=[("cv", t)])
        if dbg <= 2:
            return kb.finish()
        kb.dma("sp", dts[:, :], DTT[:, t0:t0 + SEG], writes=["dts"])
        kb.op("dve", lambda e: e.tensor_scalar(out=adts[:, :], in0=dts[:, :], scalar1=alc[:, 0:1], scalar2=None,
                                               op0=ALU.mult),
              reads=["dts", "alc"], writes=["adts"])
        kb.op("dve", lambda e: e.tensor_tensor_scan(out=css[:, :], data0=rmk[:, :], data1=adts[:, :], initial=0.0,
                                                    op0=ALU.mult, op1=ALU.add),
              reads=["rmk", "adts"], writes=["css"])
        if dbg <= 3:
            return kb.finish()
        for cl in range(SEG // L):
            c = sg * (SEG // L) + cl
            csl = slice(cl * L, (cl + 1) * L)
            for j in range(CX // 128):
                kb.op("pe", lambda e: e.transpose(out=ps_x[:, j * 128:(j + 1) * 128], in_=cv[j][:, csl], identity=ident[:, :]),
                      reads=[("cv", j), "ident"], writes=["ps_x"])
            for g in range(NG):
                tB = CX // 128 + g
                kb.op("pe", lambda e: e.transpose(out=ps_bt[:, g * N:(g + 1) * N], in_=cv[tB][:, csl], identity=ident[:, :]),
                      reads=[("cv", tB), "ident"], writes=["ps_bt"])
            if dbg <= 4:
                return kb.finish()
            pxv = ps_x[:, 0:CX].rearrange("p (h d) -> p h d", d=P)
            kb.op("dve", lambda e: e.tensor_tensor(out=xd[:, :, :], in0=pxv,
                                                   in1=dtm[:, c, :].unsqueeze(2).to_broadcast([128, NHD, P]), op=ALU.mult),
                  reads=["ps_x", "dtm"], writes=["xd"])
            kb.op("dve", lambda e: e.tensor_tensor(out=xdd[:, :, :], in0=pxv,
                                                   in1=xdsc[:, c, :].unsqueeze(2).to_broadcast([128, NHD, P]), op=ALU.mult),
                  reads=["ps_x", "xdsc"], writes=["xdd"])
            kb.op("dve", lambda e: e.tensor_tensor(out=xD[:, :, :], in0=pxv,
                                                   in1=dsk[:, :].unsqueeze(2).to_broadcast([128, NHD, P]), op=ALU.mult),
                  reads=["ps_x", "dsk"], writes=["xD"])
            kb.op("act", lambda e: e.activation(out=btm[:, :], in_=ps_bt[:, 0:NG * N], func=AF.Copy),
                  reads=["ps_bt"], writes=["btm"])
            if dbg <= 5:
                return kb.finish()
            for g in range(NG):
                tB = CX // 128 + g
                tC = CX // 128 + NG + g
                kb.op("pe", lambda e: e.matmul(ps_cb[:, g, :], lhsT=cv[tB][:, csl], rhs=cv[tC][:, csl], start=True, stop=True),
                      reads=[("cv", tB), ("cv", tC)], writes=["ps_cb"])
            kb.op("dve", lambda e: e.tensor_tensor(out=cbm[:, :, :], in0=ps_cb[:, 0:NG, :],
                                                   in1=tri[:, :].unsqueeze(1).to_broadcast([128, NG, 128]), op=ALU.mult),
                  reads=["ps_cb", "tri"], writes=["cbm"])
            if dbg <= 6:
                return kb.finish()
            for h in range(NHD):
                kb.op("pe", lambda e: e.matmul(ps_bc[h // 4][:, h % 4, :], lhsT=selh[:, h, :], rhs=css[:, csl],
                                               start=True, stop=True),
                      reads=["selh", "css"], writes=[("ps_bc", h // 4)])
            if dbg <= 7:
                return kb.finish()
            for h in range(NHD):
                g = h // HPG
                tC = CX // 128 + NG + g
                u = hi % NB
                hi += 1
                bcv = ps_bc[h // 4][:, h % 4, :]
                idx = c * NHD + h
                kb.op("dve", lambda e: e.tensor_scalar(out=Dm[u][:, :], in0=bcv, scalar1=cscol[:, idx:idx + 1], scalar2=0.0,
                                                       op0=ALU.subtract, op1=ALU.min),
                      reads=[("ps_bc", h // 4), "cscol"], writes=[("Dm", u)])
                if dbg == 7.1:
                    return kb.finish()
                kb.op("act", lambda e: e.activation(out=Em[u][:, :], in_=Dm[u][:, :], func=AF.Exp),
                      reads=[("Dm", u)], writes=[("Em", u)])
                if dbg == 7.2:
                    return kb.finish()
                kb.op("pool", lambda e: e.tensor_tensor(out=MT[u][:, :], in0=cbm[:, g, :], in1=Em[u][:, :], op=ALU.mult),
                      reads=["cbm", ("Em", u)], writes=[("MT", u)])
                if dbg == 7.3:
                    return kb.finish()
                kb.op("act", lambda e: e.activation(out=Ei[u][:, :], in_=bcv, func=AF.Exp),
                      reads=[("ps_bc", h // 4)], writes=[("Ei", u)])
                if dbg == 7.4:
                    return kb.finish()
                kb.op("pool", lambda e: e.tensor_tensor(out=Cd[u][:, :], in0=cv[tC][:, csl], in1=Ei[u][:, :], op=ALU.mult),
                      reads=[("cv", tC), ("Ei", u)], writes=[("Cd", u)])
                if dbg == 7.5:
                    return kb.finish()
                kb.op("pe", lambda e: e.matmul(ps_y[:, h * P:(h + 1) * P], lhsT=MT[u][:, :], rhs=xd[:, h, :],
                                               start=True, stop=False),
                      reads=[("MT", u), "xd"], writes=["ps_y"])
                if dbg == 7.6:
                    return kb.finish()
                kb.op("pe", lambda e: e.matmul(ps_y[:, h * P:(h + 1) * P], lhsT=Cd[u][:, :], rhs=Sbf[:, h, :],
                                               start=False, stop=True),
                      reads=[("Cd", u), "Sbf"], writes=["ps_y"])
                if dbg == 7.7 or (dbg == 7.8 and h == 3) or (dbg == 7.9 and h == 4):
                    return kb.finish()
            if dbg <= 8:
                return kb.finish()
            o = c % 2
            kb.op("dve", lambda e: e.tensor_tensor(out=yo[o][:, :], in0=ps_y[:, :],
                                                   in1=xD[:, :, :].rearrange("p h d -> p (h d)"), op=ALU.add),
                  reads=["ps_y", "xD"], writes=[("yo", o)])
            kb.dma("sp", Y[c * L:(c + 1) * L, :], yo[o][:, :], reads=[("yo", o)], writes=[("Y", c)])
            if dbg <= 9:
                return kb.finish()
            for g in range(NG):
                kb.op("pe", lambda e: e.matmul(ps_st[:, g * HPG * P:(g + 1) * HPG * P], lhsT=btm[:, g * N:(g + 1) * N],
                                               rhs=xdd[:, g * HPG:(g + 1) * HPG, :].rearrange("p h d -> p (h d)"),
                                               start=True, stop=True),
                      reads=["btm", "xdd"], writes=["ps_st"])
            kb.op("dve", lambda e: e.tensor_tensor(out=S32[:, :, :], in0=S32[:, :, :],
                                                   in1=dece[:, c, :].unsqueeze(2).to_broadcast([128, NHD, P]), op=ALU.mult),
                  reads=["S32", "dece"], writes=["S32"])
            kb.op("dve", lambda e: e.tensor_tensor(out=S32[:, :, :], in0=S32[:, :, :],
                                                   in1=ps_st[:, :].rearrange("p (h d) -> p h d", d=P), op=ALU.add),
                  reads=["S32", "ps_st"], writes=["S32"])
            kb.op("act", lambda e: e.activation(out=Sbf[:, :, :], in_=S32[:, :, :], func=AF.Copy),
                  reads=["S32"], writes=["Sbf"])
    return kb.finish()


def ssd_consts(NHD=8, SEG=2048):
    rmask = np.ones((NHD, SEG), np.float32)
    rmask[:, ::128] = 0.0
    tri = np.triu(np.ones((128, 128), np.float32))
    ones = np.ones((128, 128), np.float32)
    selh = np.zeros((NHD, NHD, 128), np.float32)
    for h in range(NHD):
        selh[h, h, :] = 1.0
    ident = np.eye(128, dtype=np.float32).astype(NPBF)
    return {"rmask": rmask, "tri": tri, "ones": ones, "selh": selh, "ident": ident}


TWO_PI = 6.283185307179586
CW1 = 6.28125
CW2 = TWO_PI - CW1


def build_s5(NJ=1024, NA=16, dbg=99):
    kb = KB()
    NG = 2 * NA
    R, H = 8, 16
    PI = 3.141592653589793
    U = kb.dram("u", [NG, 128, NJ], BF16, "ExternalInput")
    YT = kb.dram("yt", [NG, 128, NJ], BF16, "ExternalOutput")

    def load(name, shape, dt):
        dr = kb.dram(name, shape, dt, "ExternalInput")
        t = kb.sb(shape, dt)
        idx = tuple(slice(None) for _ in shape)
        kb.dma("sp", t[idx], dr[idx], writes=[name])
        return t
    lre = load("lam_re", [128, NA], F32)
    lim = load("lam_im", [128, NA], F32)
    lst = load("log_step", [128, NA], F32)
    bre = load("b_re", [128, NA, H], F32)
    bim = load("b_im", [128, NA, H], F32)
    cre = load("c_re", [128, NA, H], F32)
    cim = load("c_im", [128, NA, H], F32)
    dcol = load("dcol", [128, NG], F32)
    kv = load("kv", [128, 32], F32)
    jv = load("jv", [128, NJ], F32)
    bmask = load("bmask", [128, 128], F32)
    ident = load("ident", [128, 128], BF16)

    cnt = [0]

    def T(shape, dt=F32):
        cnt[0] += 1
        return kb.sb(shape, dt), f"t{cnt[0]}"

    tcache = {}

    def TC(shape, dt, tag):
        k = (tuple(shape), str(dt), tag)
        if k not in tcache:
            tcache[k] = T(shape, dt)
        return tcache[k]

    def dve(fn, reads, writes):
        kb.op("dve", fn, reads=reads, writes=writes)

    def act(fn, reads, writes):
        kb.op("act", fn, reads=reads, writes=writes)

    def full(t):
        return t[tuple(slice(None) for _ in t.shape)]

    def range_reduce(x, xk, shape, shift=0.0):
        tf, tfk = TC(shape, F32, "rr_tf")
        ti, tik = TC(shape, I32, "rr_ti")
        m, mk = TC(shape, F32, "rr_m")
        X, TF, TI, M = full(x), full(tf), full(ti), full(m)
        if shift != 0.0:
            dve(lambda e: e.tensor_scalar(out=X, in0=X, scalar1=shift, scalar2=None, op0=ALU.add), [xk], [xk])
        dve(lambda e: e.tensor_scalar(out=TF, in0=X, scalar1=1.0 / TWO_PI, scalar2=None, op0=ALU.mult), [xk], [tfk])
        dve(lambda e: e.tensor_copy(out=TI, in_=TF), [tfk], [tik])
        dve(lambda e: e.tensor_copy(out=TF, in_=TI), [tik], [tfk])
        dve(lambda e: e.scalar_tensor_tensor(out=X, in0=TF, scalar=-CW1, in1=X, op0=ALU.mult, op1=ALU.add), [tfk, xk], [xk])
        dve(lambda e: e.scalar_tensor_tensor(out=X, in0=TF, scalar=-CW2, in1=X, op0=ALU.mult, op1=ALU.add), [tfk, xk], [xk])
        dve(lambda e: e.tensor_scalar(out=M, in0=X, scalar1=PI, scalar2=-TWO_PI, op0=ALU.is_gt, op1=ALU.mult), [xk], [mk])
        dve(lambda e: e.tensor_tensor(out=X, in0=X, in1=M, op=ALU.add), [xk, mk], [xk])
        dve(lambda e: e.tensor_scalar(out=M, in0=X, scalar1=-PI, scalar2=TWO_PI, op0=ALU.is_lt, op1=ALU.mult), [xk], [mk])
        dve(lambda e: e.tensor_tensor(out=X, in0=X, in1=M, op=ALU.add), [xk, mk], [xk])
        dve(lambda e: e.tensor_scalar(out=X, in0=X, scalar1=PI, scalar2=-PI, op0=ALU.min, op1=ALU.max), [xk], [xk])

    step, stepk = T([128, NA])
    act(lambda e: e.activation(out=step[:, :], in_=lst[:, :], func=AF.Exp), ["log_step"], [stepk])
    th, thk = T([128, NA])
    lr, lrk = T([128, NA])
    dve(lambda e: e.tensor_tensor(out=th[:, :], in0=lim[:, :], in1=step[:, :], op=ALU.mult), ["lam_im", stepk], [thk])
    dve(lambda e: e.tensor_tensor(out=lr[:, :], in0=lre[:, :], in1=step[:, :], op=ALU.mult), ["lam_re", stepk], [lrk])
    range_reduce(th, thk, [128, NA])
    sh3 = [128, NA, 32]
    ang, angk = T(sh3)
    ang2, ang2k = T(sh3)
    mag, magk = T(sh3)
    kvb = kv[:, :].unsqueeze(1).to_broadcast(sh3)
    dve(lambda e: e.tensor_tensor(out=ang[:, :, :], in0=th[:, :].unsqueeze(2).to_broadcast(sh3), in1=kvb, op=ALU.mult),
        [thk, "kv"], [angk])
    dve(lambda e: e.tensor_copy(out=ang2[:, :, :], in_=ang[:, :, :]), [angk], [ang2k])
    dve(lambda e: e.tensor_tensor(out=mag[:, :, :], in0=lr[:, :].unsqueeze(2).to_broadcast(sh3), in1=kvb, op=ALU.mult),
        [lrk, "kv"], [magk])
    range_reduce(ang, angk, sh3)
    range_reduce(ang2, ang2k, sh3, shift=PI / 2)
    pre, prek = T(sh3)
    pim, pimk = T(sh3)
    act(lambda e: e.activation(out=pim[:, :, :], in_=ang[:, :, :], func=AF.Sin), [angk], [pimk])
    act(lambda e: e.activation(out=pre[:, :, :], in_=ang2[:, :, :], func=AF.Sin), [ang2k], [prek])
    act(lambda e: e.activation(out=mag[:, :, :], in_=mag[:, :, :], func=AF.Exp), [magk], [magk])
    dve(lambda e: e.tensor_tensor(out=pre[:, :, :], in0=pre[:, :, :], in1=mag[:, :, :], op=ALU.mult), [prek, magk], [prek])
    dve(lambda e: e.tensor_tensor(out=pim[:, :, :], in0=pim[:, :, :], in1=mag[:, :, :], op=ALU.mult), [pimk, magk], [pimk])
    nr, nrk = T([128, NA])
    den, denk = T([128, NA])
    t1, t1k = T([128, NA])
    fre, frek = T([128, NA])
    fim, fimk = T([128, NA])
    dve(lambda e: e.tensor_scalar(out=nr[:, :], in0=pre[:, :, 8], scalar1=-1.0, scalar2=None, op0=ALU.add), [prek], [nrk])
    dve(lambda e: e.tensor_tensor(out=den[:, :], in0=lre[:, :], in1=lre[:, :], op=ALU.mult), ["lam_re"], [denk])
    dve(lambda e: e.tensor_tensor(out=t1[:, :], in0=lim[:, :], in1=lim[:, :], op=ALU.mult), ["lam_im"], [t1k])
    dve(lambda e: e.tensor_tensor(out=den[:, :], in0=den[:, :], in1=t1[:, :], op=ALU.add), [denk, t1k], [denk])
    dve(lambda e: e.reciprocal(out=den[:, :], in_=den[:, :]), [denk], [denk])
    dve(lambda e: e.tensor_tensor(out=fre[:, :], in0=nr[:, :], in1=lre[:, :], op=ALU.mult), [nrk, "lam_re"], [frek])
    dve(lambda e: e.tensor_tensor(out=t1[:, :], in0=pim[:, :, 8], in1=lim[:, :], op=ALU.mult), [pimk, "lam_im"], [t1k])
    dve(lambda e: e.tensor_tensor(out=fre[:, :], in0=fre[:, :], in1=t1[:, :], op=ALU.add), [frek, t1k], [frek])
    dve(lambda e: e.tensor_tensor(out=fre[:, :], in0=fre[:, :], in1=den[:, :], op=ALU.mult), [frek, denk], [frek])
    dve(lambda e: e.tensor_tensor(out=fim[:, :], in0=pim[:, :, 8], in1=lre[:, :], op=ALU.mult), [pimk, "lam_re"], [fimk])
    dve(lambda e: e.tensor_tensor(out=t1[:, :], in0=nr[:, :], in1=lim[:, :], op=ALU.mult), [nrk, "lam_im"], [t1k])
    dve(lambda e: e.tensor_tensor(out=fim[:, :], in0=fim[:, :], in1=t1[:, :], op=ALU.subtract), [fimk, t1k], [fimk])
    dve(lambda e: e.tensor_tensor(out=fim[:, :], in0=fim[:, :], in1=den[:, :], op=ALU.mult), [fimk, denk], [fimk])

    def cmul(ore, orek, oim, oimk, are, arek, aim, aimk, bre_, brek, bim_, bimk, shape, neg_im=False):
        tt, ttk = TC(shape, F32, "cm_tt")
        TT = full(tt)
        dve(lambda e: e.tensor_tensor(out=ore, in0=are, in1=bre_, op=ALU.mult), [arek, brek], [orek])
        dve(lambda e: e.tensor_tensor(out=TT, in0=aim, in1=bim_, op=ALU.mult), [aimk, bimk], [ttk])
        dve(lambda e: e.tensor_tensor(out=ore, in0=ore, in1=TT, op=ALU.subtract), [orek, ttk], [orek])
        dve(lambda e: e.tensor_tensor(out=oim, in0=are, in1=bim_, op=ALU.mult), [arek, bimk], [oimk])
        dve(lambda e: e.tensor_tensor(out=TT, in0=aim, in1=bre_, op=ALU.mult), [aimk, brek], [ttk])
        if neg_im:
            dve(lambda e: e.scalar_tensor_tensor(out=oim, in0=oim, scalar=-1.0, in1=TT, op0=ALU.mult, op1=ALU.subtract),
                [oimk, ttk], [oimk])
        else:
            dve(lambda e: e.tensor_tensor(out=oim, in0=oim, in1=TT, op=ALU.add), [oimk, ttk], [oimk])

    sh_b = [128, NA, H]
    bbre, bbrek = T(sh_b)
    bbim, bbimk = T(sh_b)
    cmul(bbre[:, :, :], bbrek, bbim[:, :, :], bbimk,
         fre[:, :].unsqueeze(2).to_broadcast(sh_b), frek, fim[:, :].unsqueeze(2).to_broadcast(sh_b), fimk,
         bre[:, :, :], "b_re", bim[:, :, :], "b_im", sh_b)
    sh4 = [128, NA, R, H]

    def table(koff, xre, xrek, xim, ximk, neg_im=False, masked=False):
        sre, srek = TC(sh4, F32, "st_re")
        sim, simk = TC(sh4, F32, "st_im")
        cmul(sre[:, :, :, :], srek, sim[:, :, :, :], simk,
             pre[:, :, koff:koff + R].unsqueeze(3).to_broadcast(sh4), prek,
             pim[:, :, koff:koff + R].unsqueeze(3).to_broadcast(sh4), pimk,
             xre[:, :, :].unsqueeze(2).to_broadcast(sh4), xrek, xim[:, :, :].unsqueeze(2).to_broadcast(sh4), ximk,
             sh4, neg_im=neg_im)
        outs = []
        for (st, stk) in ((sre, srek), (sim, simk)):
            if not masked:
                d_, dk_ = T(sh4, BF16)
                dve(lambda e: e.tensor_copy(out=d_[:, :, :, :], in_=st[:, :, :, :]), [stk], [dk_])
                outs.append((d_, dk_))
            else:
                pair = []
                for g2 in range(2):
                    m_, mk_ = T(sh4, BF16)
                    dve(lambda e: e.tensor_copy(out=m_[:, :, :, :], in_=st[:, :, :, :]), [stk], [mk_])
                    z = slice((1 - g2) * 64, (2 - g2) * 64)
                    dve(lambda e: e.memset(m_[z, :, :, :], 0.0), [], [mk_])
                    pair.append((m_, mk_))
                outs.append(pair)
        return outs
    (wgre, wgrek), (wgim, wgimk) = table(0, bbre, bbrek, bbim, bbimk)
    care_m, caim_m = table(8, cre, "c_re", cim, "c_im", neg_im=True, masked=True)
    (lre_, lrek_), (lim_, limk_) = table(16, bbre, bbrek, bbim, bbimk)
    rre_m, rim_m = table(24, cre, "c_re", cim, "c_im", neg_im=True, masked=True)

    wg = kb.sb([128, NA, 2, 128], BF16)
    kt = kb.sb([128, NG, 128], BF16)
    ps_t = kb.ps([128, 1024], BF16)
    ps_k = kb.ps([128, 4, 128], F32)
    for a0 in range(0, NA, 4):
        for ai in range(4):
            a = a0 + ai
            for x, (src, srck) in enumerate(((wgre, wgrek), (wgim, wgimk))):
                col = (ai * 2 + x) * 128
                kb.op("pe", lambda e: e.transpose(out=ps_t[:, col:col + 128],
                                                  in_=src[:, a, :, :].rearrange("p r h -> p (r h)"),
                                                  identity=ident[:, :]),
                      reads=[srck, "ident"], writes=["ps_t"])
        dve(lambda e: e.tensor_copy(out=wg[:, a0:a0 + 4, :, :].rearrange("p a x q -> p (a x q)"), in_=ps_t[:, :]),
            ["ps_t"], ["wg"])
    if dbg <= 2:
        return kb.finish()
    for a in range(NA):
        for g2 in range(2):
            kb.op("pe", lambda e: e.matmul(ps_k[:, g2, :], lhsT=lre_[:, a, :, :].rearrange("p r h -> p (r h)"),
                                           rhs=rre_m[g2][0][:, a, :, :].rearrange("p r h -> p (r h)"), start=True, stop=False),
                  reads=[lrek_, rre_m[g2][1]], writes=["ps_k"])
            kb.op("pe", lambda e: e.matmul(ps_k[:, g2, :], lhsT=lim_[:, a, :, :].rearrange("p r h -> p (r h)"),
                                           rhs=rim_m[g2][0][:, a, :, :].rearrange("p r h -> p (r h)"), start=False, stop=True),
                  reads=[limk_, rim_m[g2][1]], writes=["ps_k"])
        dve(lambda e: e.tensor_tensor(out=kt[:, 2 * a:2 * a + 2, :], in0=ps_k[:, 0:2, :],
                                      in1=bmask[:, :].unsqueeze(1).to_broadcast([128, 2, 128]), op=ALU.mult),
            ["ps_k", "bmask"], ["kt"])

    if dbg <= 3:
        return kb.finish()
    NJC = NJ // 512
    shj = [128, NJ]
    us = [[kb.sb([128, NJ], BF16) for _ in range(2)] for _ in range(2)]
    ph, phk = T(shj)
    ph2, ph2k = T(shj)
    sn, snk = T(shj)
    cs, csk = T(shj)
    gre, grek = T(shj)
    gim, gimk = T(shj)
    tq, tqk = T(shj)
    vre, vrek = T(shj)
    vim, vimk = T(shj)
    xpre = kb.sb([128, NJ], BF16)
    xpim = kb.sb([128, NJ], BF16)
    kb.op("dve", lambda e: e.memset(xpre[:, 0:1], 0.0), writes=["xpre"])
    kb.op("dve", lambda e: e.memset(xpim[:, 0:1], 0.0), writes=["xpim"])
    ps_g = [[kb.ps([128, 512], F32) for _ in range(NJC)] for _ in range(2)]
    ps_y = [kb.ps([128, 512], F32) for _ in range(2)]
    yo = [kb.sb([128, 512], BF16) for _ in range(2)]
    oi = 0
    for a in range(NA):
        ub = a % 2
        for g2 in range(2):
            kb.dma("sp", us[ub][g2][:, :], U[2 * a + g2, :, :], writes=[("u", ub, g2)])
        dve(lambda e: e.tensor_scalar(out=ph[:, :], in0=jv[:, :], scalar1=th[:, a:a + 1], scalar2=8.0,
                                      op0=ALU.mult, op1=ALU.mult), ["jv", thk], [phk])
        dve(lambda e: e.tensor_copy(out=ph2[:, :], in_=ph[:, :]), [phk], [ph2k])
        range_reduce(ph, phk, shj)
        range_reduce(ph2, ph2k, shj, shift=PI / 2)
        act(lambda e: e.activation(out=sn[:, :], in_=ph[:, :], func=AF.Sin), [phk], [snk])
        act(lambda e: e.activation(out=cs[:, :], in_=ph2[:, :], func=AF.Sin), [ph2k], [csk])
        if dbg <= 4:
            return kb.finish()
        for x in range(2):
            for jc in range(NJC):
                for g2 in range(2):
                    kb.op("pe", lambda e: e.matmul(ps_g[x][jc][g2 * 64:(g2 + 1) * 64, :], lhsT=wg[:, a, x, g2 * 64:(g2 + 1) * 64],
                                                   rhs=us[ub][g2][:, jc * 512:(jc + 1) * 512], start=True, stop=True),
                          reads=["wg", ("u", ub, g2)], writes=[("ps_g", x, jc)])
        if dbg <= 5:
            return kb.finish()
        for jc in range(NJC):
            js = slice(jc * 512, (jc + 1) * 512)
            dve(lambda e: e.tensor_tensor(out=gre[:, js], in0=ps_g[0][jc][:, :], in1=cs[:, js], op=ALU.mult),
                [("ps_g", 0, jc), csk], [grek])
            dve(lambda e: e.tensor_tensor(out=tq[:, js], in0=ps_g[1][jc][:, :], in1=sn[:, js], op=ALU.mult),
                [("ps_g", 1, jc), snk], [tqk])
            dve(lambda e: e.tensor_tensor(out=gre[:, js], in0=gre[:, js], in1=tq[:, js], op=ALU.add), [grek, tqk], [grek])
            dve(lambda e: e.tensor_tensor(out=gim[:, js], in0=ps_g[1][jc][:, :], in1=cs[:, js], op=ALU.mult),
                [("ps_g", 1, jc), csk], [gimk])
            dve(lambda e: e.tensor_tensor(out=tq[:, js], in0=ps_g[0][jc][:, :], in1=sn[:, js], op=ALU.mult),
                [("ps_g", 0, jc), snk], [tqk])
            dve(lambda e: e.tensor_tensor(out=gim[:, js], in0=gim[:, js], in1=tq[:, js], op=ALU.subtract), [gimk, tqk], [gimk])
        if dbg <= 6:
            return kb.finish()
        m8 = mag[:, a, 15:16].to_broadcast(shj)
        dve(lambda e: e.tensor_tensor_scan(out=vre[:, :], data0=m8, data1=gre[:, :], initial=0.0, op0=ALU.mult, op1=ALU.add),
            [magk, grek], [vrek])
        dve(lambda e: e.tensor_tensor_scan(out=vim[:, :], data0=m8, data1=gim[:, :], initial=0.0, op0=ALU.mult, op1=ALU.add),
            [magk, gimk], [vimk])
        if dbg <= 7:
            return kb.finish()
        dve(lambda e: e.tensor_tensor(out=gre[:, :], in0=vre[:, :], in1=cs[:, :], op=ALU.mult), [vrek, csk], [grek])
        dve(lambda e: e.tensor_tensor(out=tq[:, :], in0=vim[:, :], in1=sn[:, :], op=ALU.mult), [vimk, snk], [tqk])
        dve(lambda e: e.tensor_tensor(out=xpre[:, 1:NJ], in0=gre[:, 0:NJ - 1], in1=tq[:, 0:NJ - 1], op=ALU.subtract),
            [grek, tqk], ["xpre"])
        dve(lambda e: e.tensor_tensor(out=gim[:, :], in0=vim[:, :], in1=cs[:, :], op=ALU.mult), [vimk, csk], [gimk])
        dve(lambda e: e.tensor_tensor(out=tq[:, :], in0=vre[:, :], in1=sn[:, :], op=ALU.mult), [vrek, snk], [tqk])
        dve(lambda e: e.tensor_tensor(out=xpim[:, 1:NJ], in0=gim[:, 0:NJ - 1], in1=tq[:, 0:NJ - 1], op=ALU.add),
            [gimk, tqk], ["xpim"])
        if dbg <= 8:
            return kb.finish()
        for g2 in range(2):
            g = 2 * a + g2
            ps_ = slice(g2 * 64, (g2 + 1) * 64)
            for jc in range(NJC):
                js = slice(jc * 512, (jc + 1) * 512)
                o = oi % 2
                oi += 1
                kb.op("pe", lambda e: e.matmul(ps_y[o][:, :], lhsT=care_m[g2][0][:, a, :, :].rearrange("p r h -> p (r h)"),
                                               rhs=xpre[:, js], start=True, stop=False),
                      reads=[care_m[g2][1], "xpre"], writes=[("ps_y", o)])
                kb.op("pe", lambda e: e.matmul(ps_y[o][:, :], lhsT=caim_m[g2][0][:, a, :, :].rearrange("p r h -> p (r h)"),
                                               rhs=xpim[:, js], start=False, stop=False),
                      reads=[caim_m[g2][1], "xpim"], writes=[("ps_y", o)])
                kb.op("pe", lambda e: e.matmul(ps_y[o][:, :], lhsT=kt[:, g, :], rhs=us[ub][g2][:, js], start=False, stop=True),
                      reads=["kt", ("u", ub, g2)], writes=[("ps_y", o)])
                dve(lambda e: e.scalar_tensor_tensor(out=yo[o][:, :], in0=us[ub][g2][:, js], scalar=dcol[:, g:g + 1],
                                                     in1=ps_y[o][:, :], op0=ALU.mult, op1=ALU.add),
                    [("u", ub, g2), "dcol", ("ps_y", o)], [("yo", o)])
                kb.dma("sp", YT[g, :, js], yo[o][:, :], reads=[("yo", o)], writes=[("YT", g, jc)])
    return kb.finish()


def s5_consts(NJ=1024):
    kvals = np.array([7, 6, 5, 4, 3, 2, 1, 0, 1, 2, 3, 4, 5, 6, 7, 8,
                      0, -1, -2, -3, -4, -5, -6, -7, 0, 1, 2, 3, 4, 5, 6, 7], np.float32)
    kv = np.broadcast_to(kvals, (128, 32)).copy()
    jv = np.broadcast_to(np.arange(1, NJ + 1, dtype=np.float32), (128, NJ)).copy()
    r = np.arange(128) // 16
    bmask = (r[:, None] <= r[None, :]).astype(np.float32)
    ident = np.eye(128, dtype=np.float32).astype(NPBF)
    return {"kv": kv, "jv": jv, "bmask": bmask, "ident": ident}


def build_rope(rows, ncols):
    kb = KB()
    X1 = kb.dram("x1", [rows, ncols], BF16, "ExternalInput")
    X2 = kb.dram("x2", [rows, ncols], BF16, "ExternalInput")
    CS = kb.dram("cos", [rows, ncols], F32, "ExternalInput")
    SN = kb.dram("sin", [rows, ncols], F32, "ExternalInput")
    O1 = kb.dram("o1", [rows, ncols], BF16, "ExternalOutput")
    O2 = kb.dram("o2", [rows, ncols], BF16, "ExternalOutput")
    CH = 2048
    x1 = [kb.sb([128, CH], BF16) for _ in range(2)]
    x2 = [kb.sb([128, CH], BF16) for _ in range(2)]
    cs = [kb.sb([128, CH], F32) for _ in range(2)]
    sn = [kb.sb([128, CH], F32) for _ in range(2)]
    t1 = [kb.sb([128, CH], F32) for _ in range(2)]
    t2 = [kb.sb([128, CH], F32) for _ in range(2)]
    o1 = [kb.sb([128, CH], BF16) for _ in range(2)]
    o2 = [kb.sb([128, CH], BF16) for _ in range(2)]
    i = 0
    for r0 in range(0, rows, 128):
        for c0 in range(0, ncols, CH):
            b = i % 2
            i += 1
            rs, cc = slice(r0, r0 + 128), slice(c0, c0 + CH)
            kb.dma("sp", x1[b][:, :], X1[rs, cc], writes=[("x1", b)])
            kb.dma("sp", x2[b][:, :], X2[rs, cc], writes=[("x2", b)])
            kb.dma("pool", cs[b][:, :], CS[rs, cc], writes=[("cs", b)])
            kb.dma("pool", sn[b][:, :], SN[rs, cc], writes=[("sn", b)])
            kb.op("dve", lambda e: e.tensor_tensor(out=t1[b][:, :], in0=x1[b][:, :], in1=cs[b][:, :], op=ALU.mult),
                  reads=[("x1", b), ("cs", b)], writes=[("t1", b)])
            kb.op("dve", lambda e: e.tensor_tensor(out=t2[b][:, :], in0=x2[b][:, :], in1=sn[b][:, :], op=ALU.mult),
                  reads=[("x2", b), ("sn", b)], writes=[("t2", b)])
            kb.op("dve", lambda e: e.tensor_tensor(out=o1[b][:, :], in0=t1[b][:, :], in1=t2[b][:, :], op=ALU.subtract),
                  reads=[("t1", b), ("t2", b)], writes=[("o1", b)])
            kb.op("dve", lambda e: e.tensor_tensor(out=t1[b][:, :], in0=x1[b][:, :], in1=sn[b][:, :], op=ALU.mult),
                  reads=[("x1", b), ("sn", b)], writes=[("t1", b)])
            kb.op("dve", lambda e: e.tensor_tensor(out=t2[b][:, :], in0=x2[b][:, :], in1=cs[b][:, :], op=ALU.mult),
                  reads=[("x2", b), ("cs", b)], writes=[("t2", b)])
            kb.op("dve", lambda e: e.tensor_tensor(out=o2[b][:, :], in0=t1[b][:, :], in1=t2[b][:, :], op=ALU.add),
                  reads=[("t1", b), ("t2", b)], writes=[("o2", b)])
            kb.dma("sp", O1[rs, cc], o1[b][:, :], reads=[("o1", b)], writes=[("O1", i)])
            kb.dma("sp", O2[rs, cc], o2[b][:, :], reads=[("o2", b)], writes=[("O2", i)])
    return kb.finish()


def emit_out_tm(kb, K, ntok, D, eps=1e-6):
    KT = K // 128
    XT = kb.dram("xt", [K, ntok], BF16, "ExternalInput")
    W = kb.dram("w_o", [K, D], F32, "ExternalInput")
    G = kb.dram("gpost", [128, D], F32, "ExternalInput")
    R = kb.dram("x", [ntok, D], F32, "ExternalInput")
    Y = kb.dram("xout", [ntok, D], F32, "ExternalOutput")
    xt = kb.sb([128, KT, ntok], BF16)
    for kt in range(KT):
        kb.dma("sp", xt[:, kt, :], XT[kt * 128:(kt + 1) * 128, :], writes=[("xt", kt)])
    wf = [kb.sb([128, D], F32) for _ in range(2)]
    wb = kb.sb([128, KT, D], BF16)
    for kt in range(KT):
        kb.dma("pool", wf[kt % 2][:, :], W[kt * 128:(kt + 1) * 128, :], writes=[("wf", kt % 2)])
        kb.op("dve", lambda e: e.tensor_copy(out=wb[:, kt, :], in_=wf[kt % 2][:, :]), reads=[("wf", kt % 2)], writes=[("wb", kt)])
    g = kb.sb([128, D], F32)
    kb.dma("sp", g[:, :], G[:, :], writes=["g"])
    NH = D // 512
    pss = [[kb.ps([128, 512], F32) for _ in range(NH)] for _ in range(2)]
    o32 = [kb.sb([128, D], F32) for _ in range(2)]
    sq = [kb.sb([128, D], F32) for _ in range(2)]
    ss = [kb.sb([128, 1], F32) for _ in range(2)]
    rs = [kb.sb([128, D], F32) for _ in range(2)]
    ys = [kb.sb([128, D], F32) for _ in range(2)]
    for t in range(ntok // 128):
        s = t % 2
        rows = slice(t * 128, (t + 1) * 128)
        kb.dma("pool", rs[s][:, :], R[rows, :], writes=[("rs", s)])
        for nh in range(NH):
            for kt in range(KT):
                kb.op("pe", lambda e: e.matmul(pss[s][nh][:, :], lhsT=xt[:, kt, rows], rhs=wb[:, kt, nh * 512:(nh + 1) * 512],
                                               start=(kt == 0), stop=(kt == KT - 1)),
                      reads=[("xt", kt), ("wb", kt)], writes=[("ps_o", s, nh)])
            kb.op("dve", lambda e: e.tensor_copy(out=o32[s][:, nh * 512:(nh + 1) * 512], in_=pss[s][nh][:, :]),
                  reads=[("ps_o", s, nh)], writes=[("o32", s)])
        kb.op("act", lambda e: e.activation(out=sq[s][:, :], in_=o32[s][:, :], func=AF.Square, accum_out=ss[s][:, :]),
              reads=[("o32", s)], writes=[("sq", s), ("ss", s)])
        kb.op("dve", lambda e: e.tensor_scalar(out=ss[s][:, :], in0=ss[s][:, :], scalar1=1.0 / D, scalar2=eps,
                                               op0=ALU.mult, op1=ALU.add), reads=[("ss", s)], writes=[("ss", s)])
        kb.op("act", lambda e: e.activation(out=ss[s][:, :], in_=ss[s][:, :], func=AF.Sqrt), reads=[("ss", s)], writes=[("ss", s)])
        kb.op("dve", lambda e: e.reciprocal(out=ss[s][:, :], in_=ss[s][:, :]), reads=[("ss", s)], writes=[("ss", s)])
        kb.op("dve", lambda e: e.scalar_tensor_tensor(out=sq[s][:, :], in0=o32[s][:, :], scalar=ss[s][:, 0:1], in1=g[:, :],
                                                      op0=ALU.mult, op1=ALU.mult),
              reads=[("o32", s), ("ss", s), "g"], writes=[("sq", s)])
        kb.op("dve", lambda e: e.tensor_tensor(out=ys[s][:, :], in0=sq[s][:, :], in1=rs[s][:, :], op=ALU.add),
              reads=[("sq", s), ("rs", s)], writes=[("ys", s)])
        kb.dma("sp", Y[rows, :], ys[s][:, :], reads=[("ys", s)], writes=[("Yout", t)])


def build_c2(ntok):
    kb = KB()
    X = kb.nc.dram_tensor("x", [ntok, 1024], F32, kind="ExternalInput").ap()
    ZB = kb.internal("i_zb", [1024, ntok], BF16)
    ZC = kb.internal("i_zc", [1024, ntok], BF16)
    GT = kb.internal("i_gt", [3072, ntok], BF16)
    M1 = kb.internal("i_m1", [1024, ntok], F32)
    TB = kb.internal("i_tb", [1024, ntok], BF16)
    M2 = kb.internal("i_m2", [1024, ntok], F32)
    MG = kb.internal("i_mg", [1024, ntok], BF16)
    with kb.scope("s1_", {"x": X, "y_zb": ZB, "y_zc": ZC, "y_gates": GT}):
        emit_linear(kb, 1024, ntok, [dict(name="zb", M=1024, func="silu"), dict(name="zc", M=1024, func="silu"),
                                     dict(name="gates", M=3072, func="sigmoid")], xnorm=True)
    with kb.scope("s2_", {"m0_a": GT[0:1024, :], "y_a": M1}):
        emit_linear(kb, 1024, ntok, [dict(name="a", M=1024, mul=["g"], odt=F32)])
    with kb.scope("s3_", {"m0_g": ZB, "y_g": TB}):
        emit_linear(kb, 1024, ntok, [dict(name="g", M=1024, func="sigmoid", bias=True, mulx=True, mul=["z"], odt=BF16)],
                    xgelu=True)
    with kb.scope("s4_", {"xt": TB, "m0_b": GT[1024:2048, :], "a_b": M1, "y_b": M2}):
        emit_linear(kb, 1024, ntok, [dict(name="b", M=1024, mul=["g"], add="m", adt=F32, odt=F32)])
    with kb.scope("s5_", {"xm": ZC, "m0_c": GT[2048:3072, :], "a_c": M2, "y_c": MG}):
        emit_linear(kb, 1024, ntok, [dict(name="c", M=1024, mul=["g"], add="m", adt=F32, odt=BF16)], xmul=True)
    with kb.scope("s6_", {"xt": MG, "x": X}):
        emit_out_tm(kb, 1024, ntok, 1024)
    return kb.finish()


D_MODEL, BATCH, SEQ, DEPTH = 1024, 4, 8192, 2
NTOK = 4096
IN_SIZES = (1024, 2048, 16, 1024, 1024, 512, 256, 32, 1024, 3072)
IN_NAMES = ("za", "xbc", "dt", "u", "zb", "cq", "ckv", "kr", "zc", "gates")

_PROGS = {}


def _prog(key, fn):
    return fn()


def _T(a):
    return np.ascontiguousarray(a.T)


def _bc(v, n=128):
    v = np.asarray(v, np.float32).reshape(1, -1)
    return np.ascontiguousarray(np.broadcast_to(v, (n, v.shape[1])))


def _lanes(x, NA=16):
    sh = x.shape[2:]
    y = x.reshape(NA, 2, 64, *sh)
    y = np.moveaxis(y, 0, 2)
    return np.ascontiguousarray(y.reshape(128, NA, *sh))


def kernel(x, pre_norm, w_in, conv_w, conv_b, dt_bias, a_log, d_ssd, ssd_norm, w_a,
           s5_log_step, s5_lambda_re, s5_lambda_im, s5_b_re, s5_b_im, s5_c_re, s5_c_im,
           s5_d, w_glu, b_glu, w_b, q_norm, w_uq, kv_norm, w_ukv, w_c, w_o, post_norm):
    f32 = np.float32
    x = np.asarray(x, f32)
    xs = [np.ascontiguousarray(x.reshape(BATCH * SEQ, D_MODEL)[c * NTOK:(c + 1) * NTOK]) for c in range(NCORES)]
    inv_freq = (10000.0 ** (-np.arange(0, 32, 2, dtype=f32) / 32)).astype(f32)
    pos = np.arange(SEQ, dtype=f32)
    ang = (pos[:, None] * inv_freq[None, :]).astype(f32)
    cosT, sinT = _T(np.cos(ang).astype(f32)), _T(np.sin(ang).astype(f32))
    rope_cs = []
    for c in range(NCORES):
        half = c % 2
        cs_ = np.zeros((384, NTOK), f32)
        sn_ = np.zeros((384, NTOK), f32)
        cs_[:272] = np.tile(cosT[:, half * NTOK:(half + 1) * NTOK], (17, 1))
        sn_[:272] = np.tile(sinT[:, half * NTOK:(half + 1) * NTOK], (17, 1))
        rope_cs.append((cs_, sn_))
    ac, sc, s5c = attn_consts(), ssd_consts(), s5_consts()

    def seqcat(lst, b):
        return np.concatenate([lst[2 * b], lst[2 * b + 1]], axis=1)

    for l in range(DEPTH):
        IDENT = np.eye(128, dtype=np.float32).astype(NPBF)
        wl = np.asarray(w_in[l], f32)
        wcols = {}
        o0 = 0
        for nm, n in zip(IN_NAMES, IN_SIZES):
            wcols[nm] = np.ascontiguousarray(wl[:, o0:o0 + n])
            o0 += n
        segs = [dict(name="za", M=1024, func="silu"), dict(name="xbc", M=2048), dict(name="dt", M=16, func="softplus", bias=True, odt=F32),
                dict(name="u", M=1024), dict(name="cq", M=512), dict(name="ckv", M=256), dict(name="kr", M=32)]
        nc = _prog("inproj", lambda: build_linear(1024, NTOK, segs, xnorm=True))
        wins = {"w_" + sg["name"]: wcols[sg["name"]] for sg in segs}
        wins["b_dt"] = np.asarray(dt_bias[l], f32).reshape(16, 1).copy()
        gpre = _bc(pre_norm[l])
        res = run_spmd(nc, [dict(x=xs[c], g=gpre, ident=IDENT, **wins) for c in range(NCORES)])
        P = {sg["name"]: [r["y_" + sg["name"]] for r in res] for sg in segs}
        del res
        nc = _prog("rms_q", lambda: build_rmsnorm(NTOK, 512, xdt=BF16, odt=BF16))
        res = run_spmd(nc, [{"x": _T(P["cq"][c]), "g": _bc(q_norm[l])} for c in range(NCORES)])
        cqn = [_T(r["y"]) for r in res]
        nc = _prog("rms_kv", lambda: build_rmsnorm(NTOK, 256, xdt=BF16, odt=BF16))
        res = run_spmd(nc, [{"x": _T(P["ckv"][c]), "g": _bc(kv_norm[l])} for c in range(NCORES)])
        ckvn = [_T(r["y"]) for r in res]
        nc = _prog("lin_q", lambda: build_linear(512, NTOK, [dict(name="q", M=1536)]))
        res = run_spmd(nc, [dict(xt=cqn[c], w_q=np.asarray(w_uq[l], f32)) for c in range(NCORES)])
        qT = [r["y_q"] for r in res]
        nc = _prog("lin_kv", lambda: build_linear(256, NTOK, [dict(name="kv", M=2048)]))
        res = run_spmd(nc, [dict(xt=ckvn[c], w_kv=np.asarray(w_ukv[l], f32)) for c in range(NCORES)])
        kvT = [r["y_kv"] for r in res]
        nc = _prog("rope", lambda: build_rope(384, NTOK))
        rin = []
        for c in range(NCORES):
            q3 = qT[c].reshape(16, 96, NTOK)
            x1 = np.zeros((384, NTOK), NPBF)
            x2 = np.zeros((384, NTOK), NPBF)
            x1[:256] = q3[:, 64:80, :].reshape(256, NTOK)
            x2[:256] = q3[:, 80:96, :].reshape(256, NTOK)
            x1[256:272] = P["kr"][c][0:16]
            x2[256:272] = P["kr"][c][16:32]
            rin.append({"x1": x1, "x2": x2, "cos": rope_cs[c][0], "sin": rope_cs[c][1]})
        res = run_spmd(nc, rin)
        ro1 = [r["o1"] for r in res]
        ro2 = [r["o2"] for r in res]
        nc = _prog("attn", lambda: build_attn(SEQ, 8, 96, 64, 96 ** -0.5))
        ains = []
        for c in range(NCORES):
            b, hh = c // 2, c % 2
            qf = seqcat(qT, b).reshape(16, 96, SEQ)
            kvf = seqcat(kvT, b).reshape(16, 128, SEQ)
            r1, r2 = seqcat(ro1, b), seqcat(ro2, b)
            hs = slice(8 * hh, 8 * hh + 8)
            QT = np.concatenate([qf[hs, 0:64], r1[:256].reshape(16, 16, SEQ)[hs], r2[:256].reshape(16, 16, SEQ)[hs]], axis=1)
            kr1 = np.broadcast_to(r1[256:272][None], (8, 16, SEQ))
            kr2 = np.broadcast_to(r2[256:272][None], (8, 16, SEQ))
            KT = np.concatenate([kvf[hs, 0:64], kr1, kr2], axis=1)
            V = np.ascontiguousarray(kvf[hs, 64:128].reshape(512, SEQ).T)
            ains.append(dict(qt=np.ascontiguousarray(QT), kt=np.ascontiguousarray(KT), v=V, **ac))
        res = run_spmd(nc, ains)
        oT = []
        for c in range(NCORES):
            b, half = c // 2, c % 2
            cols = slice(half * NTOK, (half + 1) * NTOK)
            oT.append(np.ascontiguousarray(np.concatenate(
                [res[2 * b + hh]["ot"].reshape(512, SEQ)[:, cols] for hh in range(2)], axis=0)))
        del res, ains, qT, kvT
        nc = _prog("ssd", lambda: build_ssd(SEQ))
        cwl, cbl = np.asarray(conv_w[l], f32), np.asarray(conv_b[l], f32)
        sins = []
        for c in range(NCORES):
            b, hh = c // 2, c % 2
            xf = seqcat(P["xbc"], b)
            rows = np.concatenate([np.arange(hh * 512, hh * 512 + 512), 1024 + np.arange(hh * 256, hh * 256 + 256),
                                   1536 + np.arange(hh * 256, hh * 256 + 256)])
            dtf = seqcat(P["dt"], b)[8 * hh:8 * hh + 8]
            al = np.asarray(a_log[l], f32)[8 * hh:8 * hh + 8]
            dk = np.asarray(d_ssd[l], f32)[8 * hh:8 * hh + 8]
            sins.append(dict(xbc=np.ascontiguousarray(xf[rows]),
                             convw=np.ascontiguousarray(_T(cwl[:, rows]).reshape(8, 128, 4).transpose(1, 0, 2)),
                             convb=np.ascontiguousarray(cbl[rows].reshape(8, 128, 1).transpose(1, 0, 2)),
                             dtT=np.ascontiguousarray(dtf),
                             dtm=np.ascontiguousarray(dtf.T.reshape(SEQ // 128, 128, 8).transpose(1, 0, 2)),
                             alog_col=al.reshape(8, 1).copy(), alog_bc=_bc(al), dskip_bc=_bc(dk), **sc))
        res = run_spmd(nc, sins)
        yssd = []
        for c in range(NCORES):
            b, half = c // 2, c % 2
            rows = slice(half * NTOK, (half + 1) * NTOK)
            yssd.append(np.ascontiguousarray(np.concatenate([res[2 * b + hh]["y"][rows] for hh in range(2)], axis=1)))
        del res, sins
        nc = _prog("s5", lambda: build_s5(SEQ // 8, 16))
        eins = []
        for c in range(NCORES):
            b, hh = c // 2, c % 2
            uf = seqcat(P["u"], b)[hh * 512:(hh + 1) * 512]
            U = np.ascontiguousarray(uf.reshape(32, 16, SEQ // 8, 8).transpose(0, 3, 1, 2).reshape(32, 128, SEQ // 8))
            gs = slice(32 * hh, 32 * hh + 32)
            dg = np.asarray(s5_d[l], f32).reshape(64, 16)[gs]
            eins.append(dict(u=U, lam_re=_lanes(np.asarray(s5_lambda_re[l], f32)[gs]),
                             lam_im=_lanes(np.asarray(s5_lambda_im[l], f32)[gs]),
                             log_step=_lanes(np.broadcast_to(np.asarray(s5_log_step[l], f32)[gs][:, None], (32, 64)).copy()),
                             b_re=_lanes(np.asarray(s5_b_re[l], f32)[gs]), b_im=_lanes(np.asarray(s5_b_im[l], f32)[gs]),
                             c_re=_lanes(np.asarray(s5_c_re[l], f32)[gs].transpose(0, 2, 1)),
                             c_im=_lanes(np.asarray(s5_c_im[l], f32)[gs].transpose(0, 2, 1)),
                             dcol=np.ascontiguousarray(np.tile(dg.T, (8, 1))), **s5c))
        res = run_spmd(nc, eins)
        ys5T = []
        for c in range(NCORES):
            b, half = c // 2, c % 2
            cols = slice(half * NTOK, (half + 1) * NTOK)
            parts = []
            for hh in range(2):
                yt = res[2 * b + hh]["yt"].reshape(32, 8, 16, SEQ // 8).transpose(0, 2, 3, 1).reshape(512, SEQ)
                parts.append(yt[:, cols])
            ys5T.append(np.ascontiguousarray(np.concatenate(parts, axis=0)))
        del res, eins
        nc = _prog("rms_ssd", lambda: build_rmsnorm(NTOK, 1024, xdt=BF16, premul=True, odt=BF16))
        res = run_spmd(nc, [{"x": yssd[c], "pm": _T(P["za"][c]), "g": _bc(ssd_norm[l])} for c in range(NCORES)])
        ynT = [_T(r["y"]) for r in res]
        del res, yssd
        nc = _prog("c2", lambda: build_c2(NTOK))
        cw = {"s1_g": gpre, "s1_ident": IDENT, "s1_w_zb": wcols["zb"], "s1_w_zc": wcols["zc"], "s1_w_gates": wcols["gates"],
              "s2_w_a": np.asarray(w_a[l], f32), "s3_w_g": np.asarray(w_glu[l], f32),
              "s3_b_g": np.asarray(b_glu[l], f32).reshape(1024, 1).copy(), "s4_w_b": np.asarray(w_b[l], f32),
              "s5_w_c": np.asarray(w_c[l], f32), "s6_w_o": np.asarray(w_o[l], f32), "s6_gpost": _bc(post_norm[l])}
        res = run_spmd(nc, [dict(x=xs[c], s2_xt=ynT[c], s3_xt=ys5T[c], s5_xt=oT[c], **cw) for c in range(NCORES)])
        xs = [r["s6_xout"] for r in res]
        del res, ynT, ys5T, oT, P
    return np.concatenate(xs, axis=0).reshape(BATCH, SEQ, D_MODEL).astype(np.float32)
```
